# Optimizing a Trainium2 kernel written in Bass

```python
import jax, jax.numpy as jnp
from jax import lax
import numpy as np

D_MODEL = 2048
BATCH = 16
SEQ = 2048
DEPTH = 1
DEC_BATCH = 8
DEC_SEQ = 32
PAST_LEN = 1024

CHUNK = 64
D_MIX = D_MODEL
D_A = D_MIX // 2
D_B = D_MIX - D_A
N_GROUPS_A = 8
HEAD_DIM_A = D_A // N_GROUPS_A
MLP_CHUNK = 128
CONV_WIDTH = 31
D_FF = 5632
N_MOD = 9
EPS = 1e-6

kernel_name = "hybrid_gmlp_conformer_stream_step"


def rms_norm(x, g):
    xf = x.astype(jnp.float32)
    y = xf * lax.rsqrt(jnp.mean(xf * xf, axis=-1, keepdims=True) + EPS)
    return (y * g.astype(jnp.float32)).astype(x.dtype)


def layer_norm(x, g, b):
    xf = x.astype(jnp.float32)
    mu = jnp.mean(xf, axis=-1, keepdims=True)
    xc = xf - mu
    y = xc * lax.rsqrt(jnp.mean(xc * xc, axis=-1, keepdims=True) + EPS)
    return (y * g.astype(jnp.float32) + b.astype(jnp.float32)).astype(x.dtype)


def swiglu_ffn(h, w_up, w_down):
    gu = h @ w_up
    g, u = jnp.split(gu, 2, axis=-1)
    return (jax.nn.silu(g) * u) @ w_down


def depthwise_causal_conv(xc, w_dw, b_dw):
    out = lax.conv_general_dilated(
        xc, w_dw[:, None, :].astype(xc.dtype), window_strides=(1,), padding='VALID',
        dimension_numbers=('NWC', 'WIO', 'NWC'), feature_group_count=xc.shape[-1])
    return out + b_dw


def _mixing(h, conv_hist, chunk_len, w_in, g_v, w_s_masked, b_s, w_dw, b_dw, g_cn, b_cn,
            g_out_a, g_out_b, w_out):
    z = h @ w_in
    u, v, ga, gb = jnp.split(z, [D_A, 2 * D_A, 2 * D_A + D_B], axis=-1)
    bsz, L, _ = v.shape
    v = rms_norm(v, g_v)
    vr = v.reshape(bsz, L // chunk_len, chunk_len, N_GROUPS_A, HEAD_DIM_A)
    ws = w_s_masked[:, :chunk_len, :chunk_len]
    bias = b_s[:, :chunk_len].T[None, None, :, :, None]
    sp = jnp.einsum('gij,bnjgd->bnigd', ws, vr) + bias
    y_a = u * sp.reshape(bsz, L, D_A)
    glu = ga * jax.nn.sigmoid(gb)
    if conv_hist is None:
        conv_hist = jnp.zeros((bsz, CONV_WIDTH - 1, D_B), glu.dtype)
    xc = jnp.concatenate([conv_hist.astype(glu.dtype), glu], axis=1)
    y_b = jax.nn.silu(layer_norm(depthwise_causal_conv(xc, w_dw, b_dw), g_cn, b_cn))
    new_conv = xc[:, -(CONV_WIDTH - 1):]
    y = jnp.concatenate([rms_norm(y_a, g_out_a), rms_norm(y_b, g_out_b)], axis=-1) @ w_out
    return y, new_conv, v


def _layer(x, c, conv_hist, chunk_len, w_ada, b_ada, g_ffn1, w_up1, w_down1, g_mix, w_in, g_v,
           w_s_masked, b_s, w_dw, b_dw, g_cn, b_cn, g_out_a, g_out_b, w_out, g_ffn2, w_up2, w_down2):
    mod = jax.nn.silu(c) @ w_ada + b_ada
    sh1, sc1, gt1, sh2, sc2, gt2, sh3, sc3, gt3 = [m[:, None, :] for m in jnp.split(mod, N_MOD, axis=-1)]
    h = rms_norm(x, g_ffn1) * (1 + sc1) + sh1
    x = x + 0.5 * gt1 * swiglu_ffn(h, w_up1, w_down1)
    h = rms_norm(x, g_mix) * (1 + sc2) + sh2
    y, new_conv, v = _mixing(h, conv_hist, chunk_len, w_in, g_v, w_s_masked, b_s, w_dw, b_dw,
                             g_cn, b_cn, g_out_a, g_out_b, w_out)
    x = x + gt2 * y
    h = rms_norm(x, g_ffn2) * (1 + sc3) + sh3
    x = x + 0.5 * gt3 * swiglu_ffn(h, w_up2, w_down2)
    return x, new_conv, v


def setup_inputs(seed: int = 0) -> dict:
    key = jax.random.key(seed)
    ks = jax.random.split(key, 32)
    n = lambda k, shape, s: jax.random.normal(k, shape, jnp.float32) * s
    L = DEPTH
    D = D_MODEL
    return {
        "x_prompt": n(ks[0], (BATCH, SEQ, D), 1.0),
        "x_sample": n(ks[1], (DEC_BATCH, DEC_SEQ, D), 1.0),
        "cache_conv": n(ks[2], (L, DEC_BATCH, CONV_WIDTH - 1, D_B), 0.5),
        "c_prompt": n(ks[3], (BATCH, D), 1.0),
        "c_sample": n(ks[4], (DEC_BATCH, D), 1.0),
        "w_ada": n(ks[5], (L, D, N_MOD * D), 0.5 * D ** -0.5),
        "b_ada": n(ks[6], (L, N_MOD * D), 0.02),
        "g_ffn1": 1.0 + n(ks[7], (L, D), 0.02),
        "w_up1": n(ks[8], (L, D, 2 * D_FF), D ** -0.5),
        "w_down1": n(ks[9], (L, D_FF, D), D_FF ** -0.5),
        "g_mix": 1.0 + n(ks[10], (L, D), 0.02),
        "w_in": n(ks[11], (L, D, 2 * D_A + 2 * D_B), D ** -0.5),
        "g_v": 1.0 + n(ks[12], (L, D_A), 0.02),
        "w_s": n(ks[13], (L, N_GROUPS_A, MLP_CHUNK, MLP_CHUNK), MLP_CHUNK ** -0.5),
        "b_s": n(ks[14], (L, N_GROUPS_A, MLP_CHUNK), 0.02),
        "w_dw": n(ks[15], (L, CONV_WIDTH, D_B), CONV_WIDTH ** -0.5),
        "b_dw": n(ks[16], (L, D_B), 0.02),
        "g_cn": 1.0 + n(ks[17], (L, D_B), 0.02),
        "b_cn": n(ks[18], (L, D_B), 0.02),
        "g_out_a": 1.0 + n(ks[19], (L, D_A), 0.02),
        "g_out_b": 1.0 + n(ks[20], (L, D_B), 0.02),
        "w_out": n(ks[21], (L, D_MIX, D), D_MIX ** -0.5),
        "g_ffn2": 1.0 + n(ks[22], (L, D), 0.02),
        "w_up2": n(ks[23], (L, D, 2 * D_FF), D ** -0.5),
        "w_down2": n(ks[24], (L, D_FF, D), D_FF ** -0.5),
        "g_final": 1.0 + n(ks[25], (D,), 0.02),
    }


def reference(x_prompt, x_sample, cache_conv, c_prompt, c_sample, w_ada, b_ada, g_ffn1, w_up1,
              w_down1, g_mix, w_in, g_v, w_s, b_s, w_dw, b_dw, g_cn, b_cn, g_out_a, g_out_b, w_out,
              g_ffn2, w_up2, w_down2, g_final):
    tri = jnp.tril(jnp.ones((MLP_CHUNK, MLP_CHUNK), dtype=w_s.dtype))
    sample_len = x_sample.shape[1]
    xp, xs = x_prompt, x_sample
    conv_p, conv_s, v_s = [], [], []
    for l in range(DEPTH):
        p = (w_ada[l], b_ada[l], g_ffn1[l], w_up1[l], w_down1[l], g_mix[l], w_in[l], g_v[l],
             w_s[l] * tri, b_s[l], w_dw[l], b_dw[l], g_cn[l], b_cn[l], g_out_a[l], g_out_b[l],
             w_out[l], g_ffn2[l], w_up2[l], w_down2[l])
        xp, cp, _ = _layer(xp, c_prompt, None, MLP_CHUNK, *p)
        xs, cs, vs = _layer(xs, c_sample, cache_conv[l], sample_len, *p)
        conv_p.append(cp)
        conv_s.append(cs)
        v_s.append(vs)
    y_prompt = rms_norm(xp, g_final)
    y_sample = rms_norm(xs, g_final)
    return (y_prompt, y_sample, jnp.stack(conv_p), jnp.stack(conv_s), jnp.stack(v_s))
```

```python
import numpy as np
from contextlib import ExitStack
import concourse.bass as bass
import concourse.mybir as mybir
from concourse.bass_utils import run_bass_kernel_spmd

F32 = mybir.dt.float32
BF16 = mybir.dt.bfloat16
AF = mybir.ActivationFunctionType
ALU = mybir.AluOpType

D = 2048
KC = 16
DFF = 5632
DA = 1024
DB = 1024
NG = 8
CW = 31
HALO = 30
EPS = 1e-6
W = 512
DEC = 32
NSLOT = 4
SLOT = 8192
NTM = 2
NMOD = 9
N_CORES = 8
POOL_FROM_TILE = 2


class Sched:
    def __init__(self, nc, es):
        self.nc = nc
        self.E = {"pe": nc.tensor, "act": nc.scalar, "dve": nc.vector, "pool": nc.gpsimd, "sp": nc.sync}
        self.sems = {}
        self.val = {}
        for e in ("pe", "act", "dve", "pool"):
            self.sems[e] = es.enter_context(nc.semaphore("sem_" + e))
            self.val[e] = 0
        self.es = es
        self.known = {e: {} for e in self.E}
        self.lastw = {}
        self.rd = {}
        self.n_wait = 0

    def new_sem(self, name):
        self.sems[name] = self.es.enter_context(self.nc.semaphore(name))
        self.val[name] = 0
        return name

    def _deps(self, reads, writes):
        need = {}

        def add(tok):
            s, v = tok
            if need.get(s, 0) < v:
                need[s] = v

        for r in reads:
            t = self.lastw.get(r)
            if t:
                add(t)
        for w in writes:
            t = self.lastw.get(w)
            if t:
                add(t)
            for s, v in self.rd.get(w, {}).items():
                add((s, v))
        return need

    def _record(self, tok, reads, writes):
        s, v = tok
        for r in reads:
            d = self.rd.setdefault(r, {})
            if d.get(s, 0) < v:
                d[s] = v
        for w in writes:
            self.lastw[w] = tok
            self.rd[w] = {}

    def _emit_waits(self, e, need, embed_ok):
        k = self.known[e]
        todo = []
        for s, v in need.items():
            if e == "pe" and s == "pe":
                continue
            if k.get(s, 0) >= v:
                continue
            todo.append((s, v))
            k[s] = v
        emb = None
        if embed_ok and todo:
            emb = todo.pop()
        for s, v in todo:
            self.E[e].wait_ge(self.sems[s], v)
            self.n_wait += 1
        return emb

    def op(self, e, fn, reads=(), writes=(), signal=True, extra=None):
        need = self._deps(reads, writes)
        if extra:
            for s, v in extra:
                if need.get(s, 0) < v:
                    need[s] = v
        emb = self._emit_waits(e, need, embed_ok=(e != "pe"))
        ins = fn()
        if emb is not None:
            ins.wait_op(self.sems[emb[0]], emb[1], "sem-ge")
        if signal:
            ins.then_inc(self.sems[e], 1)
            self.val[e] += 1
            tok = (e, self.val[e])
        else:
            tok = (e, self.val[e] + 1)
        self._record(tok, reads, writes)
        return tok

    def dma(self, q, out, in_, sem, reads=(), writes=(), extra=None):
        need = self._deps(reads, writes)
        if extra:
            for s, v in extra:
                if need.get(s, 0) < v:
                    need[s] = v
        self._emit_waits(q, need, embed_ok=False)
        ins = self.E[q].dma_start(out=out, in_=in_)
        ins.then_inc(self.sems[sem], 16)
        self.val[sem] += 16
        tok = (sem, self.val[sem])
        self._record(tok, reads, writes)
        return tok

    def wait_tok(self, e, toks):
        need = {}
        for s, v in toks:
            if need.get(s, 0) < v:
                need[s] = v
        self._emit_waits(e, need, embed_ok=False)


def block_list():
    bl = []
    for f in (1, 2):
        ffn = []
        for half in range(2):
            for bi in range(11):
                ffn.append(("UP", f, half * 11 + bi))
            for mb in range(8):
                ffn.append(("DN", f, half, mb))
        if f == 1:
            bl += ffn
            for nb in (2, 3, 0, 1):
                bl.append(("GAB", nb))
            for ub in range(2):
                bl.append(("INU", ub))
            for vb in range(2):
                bl.append(("INV", vb))
            for ob in range(4):
                bl.append(("OUT", ob))
        else:
            bl += ffn
    return bl


def build(S=2048, n_conv_sems=8, pool_from=POOL_FROM_TILE):
    NTB = S // W
    nc = bass.Bass("TRN2", target_bir_lowering=False, dynamic_dma_scratch_size=8192)
    dt = nc.dram_tensor
    xp = dt("xp", [2, S, D], F32, kind="ExternalInput").ap()
    xs = dt("xs", [DEC, D], F32, kind="ExternalInput").ap()
    cache = dt("cache", [HALO, DB], F32, kind="ExternalInput").ap()
    c3t = dt("c3t", [128, KC * 3], F32, kind="ExternalInput").ap()
    w_ada = dt("w_ada", [D, NMOD * D], F32, kind="ExternalInput").ap()
    b_ada = dt("b_ada", [128, NMOD * KC], F32, kind="ExternalInput").ap()
    gd = dt("gd", [128, 4 * KC], F32, kind="ExternalInput").ap()
    g8 = dt("g8", [128, 5 * 8], F32, kind="ExternalInput").ap()
    wdw = dt("wdw", [128, 8 * CW], F32, kind="ExternalInput").ap()
    gvb = dt("gvb", [128, DA], F32, kind="ExternalInput").ap()
    wst = dt("wst", [128, NG * 128], F32, kind="ExternalInput").ap()
    trilT = dt("trilT", [128, 128], F32, kind="ExternalInput").ap()
    bs = dt("bs", [1, NG * 128], F32, kind="ExternalInput").ap()
    ident_d = dt("ident", [128, 128], F32, kind="ExternalInput").ap()
    w_up = {1: dt("w_up1", [D, 2 * DFF], F32, kind="ExternalInput").ap(),
            2: dt("w_up2", [D, 2 * DFF], F32, kind="ExternalInput").ap()}
    w_dn = {1: dt("w_down1", [DFF, D], F32, kind="ExternalInput").ap(),
            2: dt("w_down2", [DFF, D], F32, kind="ExternalInput").ap()}
    w_in = dt("w_in", [D, 4 * DA], F32, kind="ExternalInput").ap()
    w_out = dt("w_out", [D, D], F32, kind="ExternalInput").ap()
    yp = dt("yp", [2, S, D], F32, kind="ExternalOutput").ap()
    ys = dt("ys", [DEC, D], F32, kind="ExternalOutput").ap()
    scp = dt("scp", [2, HALO, DB], F32, kind="ExternalOutput").ap()
    scs = dt("scs", [HALO, DB], F32, kind="ExternalOutput").ap()
    svs = dt("svs", [DEC, DA], F32, kind="ExternalOutput").ap()

    blocks = block_list()
    NBLK = len(blocks)
    scr = dt("scr", [NBLK, 128, SLOT], BF16, kind="Internal").ap()

    def conv_srcs(b, base=None):
        blk = blocks[b]
        if base is None:
            base = scr[b]
        kind = blk[0]
        out = []
        if kind == "UP":
            _, f, bi = blk
            src = w_up[f].rearrange("(k p) (gu j c) -> p k gu j c", p=128, gu=2, c=256)
            dst = base.rearrange("p (k gu c) -> p k gu c", k=KC, gu=2)
            for gu in range(2):
                out.append((dst[:, :, gu, :], src[:, :, gu, bi, :]))
        elif kind == "DN":
            _, f, half, mb = blk
            src = w_dn[f].rearrange("(hh k p) (mb c) -> p hh k mb c", p=128, k=22, c=256)
            dst = base[:, 0:22 * 256].rearrange("p (k c) -> p k c", c=256)
            out.append((dst, src[:, half, :, mb, :]))
        elif kind == "GAB":
            _, nb = blk
            src = w_in.rearrange("(k p) (sec j c) -> p k sec j c", p=128, sec=4, c=256)
            dst = base.rearrange("p (k gu c) -> p k gu c", k=KC, gu=2)
            for gu in range(2):
                out.append((dst[:, :, gu, :], src[:, :, 2 + gu, nb, :]))
        elif kind in ("INU", "INV"):
            j = blk[1] + (0 if kind == "INU" else 2)
            src = w_in.rearrange("(k p) (j c) -> p k j c", p=128, c=512)
            dst = base.rearrange("p (k c) -> p k c", c=512)
            out.append((dst, src[:, :, j, :]))
        elif kind == "OUT":
            src = w_out.rearrange("(k p) (j c) -> p k j c", p=128, c=512)
            dst = base.rearrange("p (k c) -> p k c", c=512)
            out.append((dst, src[:, :, blk[1], :]))
        return out

    with ExitStack() as es:
        sb = lambda name, shape, dtype: es.enter_context(nc.sbuf_tensor(name, shape, dtype))
        XF = sb("XF", [128, KC, W], F32)
        H = sb("H", [128, KC, W], BF16)
        BIGA = sb("BIGA", [128, 8192], F32)
        HID = BIGA.bitcast(BF16)
        UB = sb("UB", [128, 8, W], BF16)
        VN = sb("VN", [128, 4, DA], BF16)
        GLU = sb("GLU", [128, 8, HALO + W], F32)
        WP = sb("WP", [128, NSLOT, SLOT], BF16)
        SQ = sb("SQ", [128, 4, W], BF16)
        TMP = sb("TMP", [128, 2, W], F32)
        SG = sb("SG", [128, 2, W], F32)
        XB = sb("XB", [128, 2, W], BF16)
        GLUB = sb("GLUB", [128, 4, HALO + W], BF16)
        NDG = 4
        DG = sb("DG", [128, NDG, 128], BF16)
        IDB = sb("IDB", [128, 128], BF16)
        MOD = TMP[:, 0, 0:NMOD * KC * 3].rearrange("p (c r) -> p c r", r=3)
        GS = sb("GS", [128, NMOD * KC, 3], F32)
        GD = sb("GD", [128, 4 * KC], F32)
        G8 = sb("G8", [128, 5 * 8], F32)
        WDW = sb("WDW", [128, 8 * CW], F32)
        GVB = sb("GVB", [128, DA], F32)
        WST = sb("WST", [128, NG, 128], BF16)
        TRI = SG[:, 0, 0:128]
        BS128 = sb("BS128", [128, NG * 128], BF16)
        IDENT = sb("IDENT", [128, 128], F32)
        ONES = sb("ONES", [128, 128], BF16)
        C3 = sb("C3", [128, KC * 3], F32)
        C3B = sb("C3B", [128, KC * 3], BF16)
        BADA = sb("BADA", [128, NMOD * KC], F32)
        SS = sb("SS", [128, 8], F32)
        NEGH = sb("NEGH", [128, 1], F32)
        PS = [es.enter_context(nc.psum_tensor("ps%d" % i, [128, W], F32)) for i in range(8)]

        sc = Sched(nc, es)
        op = sc.op
        slot_sem = [sc.new_sem("s_slot%d" % i) for i in range(NSLOT)]
        slot_sem_sw = [sc.new_sem("s_slotsw%d" % i) for i in range(NSLOT)]
        tmi_sem = [sc.new_sem("s_tmi%d" % i) for i in range(NTM)]
        tmo_sem = [sc.new_sem("s_tmo%d" % i) for i in range(6)]
        cst_sem = sc.new_sem("s_cst")
        store_sem = [sc.new_sem("s_store%d" % i) for i in range(NSLOT)]
        misc_sem = sc.new_sem("s_misc")

        cst_toks = []
        for dst, src in ((C3, c3t), (BADA, b_ada), (GD, gd), (G8, g8), (WDW, wdw), (GVB, gvb),
                         (IDENT, ident_d)):
            cst_toks.append(sc.dma("sp", dst[:], src, cst_sem))
        cst_toks.append(sc.dma("sp", TRI, trilT, cst_sem, writes=[("SG", 0)]))
        BSF = TMP[0:1, :, :].rearrange("p a b -> p (a b)")
        cst_toks.append(sc.dma("sp", BSF, bs, cst_sem, writes=[("TMP", 0), ("TMP", 1)]))
        cst_toks.append(sc.dma("sp", BIGA[:, 0:NG * 128], wst, cst_sem, writes=[("BA", i) for i in range(4)]))
        cst_all = [cst_toks[-1]]
        st = []
        st.append(op("dve", lambda: nc.vector.memset(ONES[:], 1.0), extra=cst_all, writes=[("ONES",)]))
        st.append(op("dve", lambda: nc.vector.memset(BS128[:], 0.0), writes=[("BS128",)]))
        st.append(op("dve", lambda: nc.vector.memset(NEGH[:], -0.5), writes=[("NEGH",)]))
        st.append(op("dve", lambda: nc.vector.tensor_copy(out=IDB[:], in_=IDENT[:]), writes=[("IDB",)]))
        st.append(op("dve", lambda: nc.vector.tensor_copy(out=BS128[0:1, :], in_=BSF),
                     reads=[("TMP", 0), ("TMP", 1)], writes=[("BS128",)]))
        for g in range(NG):
            st.append(op("dve", lambda g=g: nc.vector.tensor_tensor(
                out=WST[:, g, :], in0=BIGA[:, g * 128:(g + 1) * 128], in1=TRI, op=ALU.mult),
                reads=[("BA", i) for i in range(4)] + [("SG", 0)], writes=[("WST", g)]))
        st.append(op("act", lambda: nc.scalar.activation(out=C3B[:], in_=C3[:], func=AF.Silu), extra=cst_all,
                     writes=[("C3B",)]))

        ring = {"n": 0}
        pe_ring = {"n": 0}

        def slot_of(seq):
            return seq % NSLOT

        NADA = (NMOD * D) // 512
        wa_v = w_ada.rearrange("(k p) (j c) -> p k j c", p=128, c=512)
        ada_tok = None
        for j in range(NADA):
            seq = ring["n"]; ring["n"] += 1
            s = slot_of(seq)
            sc.dma("pool", WP[:, s, :].rearrange("p (k c) -> p k c", c=512), wa_v[:, :, j, :],
                   slot_sem_sw[s], writes=[("WP", s)])
            for m4 in range(4):
                col = j * 4 + m4
                for k in range(KC):
                    ada_tok = op("pe", lambda s=s, k=k, m4=m4, col=col: nc.tensor.matmul(
                        PS[0][:, col * 3:(col + 1) * 3],
                        lhsT=WP[:, s, k * 512 + m4 * 128:k * 512 + (m4 + 1) * 128],
                        rhs=C3B[:, k * 3:(k + 1) * 3], start=(k == 0), stop=(k == KC - 1)),
                        reads=[("WP", s), ("C3B",)], writes=[("PS", 0)],
                        signal=(k == KC - 1 and m4 == 3), extra=st if (j == 0 and m4 == 0 and k == 0) else None)
        for r in range(3):
            op("dve", lambda r=r: nc.vector.tensor_tensor(
                out=MOD[:, :, r], in0=PS[0][:, 0:NMOD * KC * 3].rearrange("p (c r) -> p c r", r=3)[:, :, r],
                in1=BADA[:], op=ALU.add), reads=[("PS", 0)], writes=[("MOD",), ("TMP", 0)])
        for s3 in range(3):
            gsel = GD[:, (0 if s3 == 0 else (1 if s3 == 1 else 2)) * KC:(0 if s3 == 0 else (1 if s3 == 1 else 2)) * KC + KC]
            for r in range(3):
                base = 3 * s3 * KC
                op("dve", lambda base=base, r=r: nc.vector.tensor_copy(
                    out=GS[:, base:base + KC, r], in_=MOD[:, base:base + KC, r]),
                    reads=[("MOD",), ("TMP", 0)], writes=[("GS",)])
                op("dve", lambda base=base, r=r, gsel=gsel: nc.vector.scalar_tensor_tensor(
                    out=GS[:, base + KC:base + 2 * KC, r], in0=MOD[:, base + KC:base + 2 * KC, r], scalar=1.0,
                    in1=gsel, op0=ALU.add, op1=ALU.mult), reads=[("MOD",)], writes=[("GS",)])
                op("dve", lambda base=base, r=r, s3=s3: nc.vector.tensor_scalar(
                    out=GS[:, base + 2 * KC:base + 3 * KC, r], in0=MOD[:, base + 2 * KC:base + 3 * KC, r],
                    scalar1=(1.0 if s3 == 1 else 0.5), scalar2=None, op0=ALU.mult),
                    reads=[("MOD",), ("TMP", 0)], writes=[("GS",)])
        gs_tok = sc.lastw[("GS",)]
        for e in ("act", "dve", "pe", "pool"):
            sc.wait_tok(e, st + [gs_tok])

        def SH(s3, k, r):
            return GS[:, 3 * s3 * KC + k, r:r + 1]

        def GG(s3, k, r):
            return GS[:, (3 * s3 + 1) * KC + k, r:r + 1]

        def GT(s3, k, r):
            return GS[:, (3 * s3 + 2) * KC + k, r:r + 1]

        st_rot = {"n": 0}
        mm_rot = {"n": 0}
        sq_rot = {"n": 0}
        tmp_rot = {"n": 0}
        sg_rot = {"n": 0}
        tm_rot = {"n": 0}
        cvb_rot = {"n": 0}

        def mm_bank():
            b = mm_rot["n"] % 4
            mm_rot["n"] += 1
            return b

        def load_block(tile_idx, bidx):
            seq = ring["n"]; ring["n"] += 1
            s = slot_of(seq)
            n = 5632 if blocks[bidx][0] == "DN" else SLOT
            owner = 0 if 3 * bidx < NBLK else 1
            if tile_idx <= owner:
                for dst, src in conv_srcs(bidx, WP[:, s, :]):
                    sc.dma("pool", dst, src, slot_sem_sw[s], writes=[("WP", s)])
                if tile_idx == owner:
                    sc.dma("sp", scr[bidx][:, 0:n], WP[:, s, 0:n], store_sem[s], reads=[("WP", s)],
                           writes=[("SCR", bidx)])
            else:
                sc.dma("sp", WP[:, s, 0:n], scr[bidx][:, 0:n], slot_sem[s], reads=[("SCR", bidx)],
                       writes=[("WP", s)])
            return s

        def rstd_from_sum(bank, wd, n_feat, np_=128):
            op("act", lambda: nc.scalar.activation(out=PS[bank][0:np_, 0:wd], in_=PS[bank][0:np_, 0:wd],
                                                   func=AF.Sqrt, bias=EPS, scale=1.0 / n_feat),
               reads=[("PS", bank)], writes=[("PS", bank)])
            op("dve", lambda: nc.vector.reciprocal(out=PS[bank][0:np_, 0:wd], in_=PS[bank][0:np_, 0:wd]),
               reads=[("PS", bank)], writes=[("PS", bank)])

        def sumsq(srcs, wd, bank=None):
            if bank is None:
                bank = 4 + (st_rot["n"] % 2)
                st_rot["n"] += 1
            n = len(srcs)
            for i, (ap, regs) in enumerate(srcs):
                q = sq_rot["n"] % 4
                sq_rot["n"] += 1
                op("act", lambda ap=ap, q=q: nc.scalar.activation(out=SQ[:, q, 0:wd], in_=ap, func=AF.Square),
                   reads=regs, writes=[("SQ", q)])
                op("pe", lambda q=q, i=i: nc.tensor.matmul(PS[bank][:, 0:wd], lhsT=ONES[:], rhs=SQ[:, q, 0:wd],
                                                           start=(i == 0), stop=(i == n - 1)),
                   reads=[("SQ", q)], writes=[("PS", bank)], signal=True)
            return bank

        class StatAcc:
            def __init__(self, wd, n_total, lag=2):
                self.bank = 4 + (st_rot["n"] % 2)
                st_rot["n"] += 1
                self.wd = wd
                self.n = n_total
                self.i = 0
                self.lag = lag
                self.pending = []

            def add(self, ap, regs):
                q = sq_rot["n"] % 4
                sq_rot["n"] += 1
                wd_ = self.wd
                op("act", lambda: nc.scalar.activation(out=SQ[:, q, 0:wd_], in_=ap, func=AF.Square),
                   reads=regs, writes=[("SQ", q)])
                self.pending.append(q)
                if len(self.pending) > self.lag:
                    self._mm(self.pending.pop(0))

            def _mm(self, q):
                i, n, bank, wd_ = self.i, self.n, self.bank, self.wd
                op("pe", lambda: nc.tensor.matmul(PS[bank][:, 0:wd_], lhsT=ONES[:], rhs=SQ[:, q, 0:wd_],
                                                  start=(i == 0), stop=(i == n - 1)),
                   reads=[("SQ", q)], writes=[("PS", bank)], signal=True)
                self.i += 1

            def finish(self):
                while self.pending:
                    self._mm(self.pending.pop(0))
                assert self.i == self.n
                return self.bank

        def norm_mod(s3, r, wd, bank=None):
            if bank is None:
                bank = sumsq([(XF[:, f, 0:wd], [("XF", f)]) for f in range(KC)], wd)
            rstd_from_sum(bank, wd, D)
            for f in range(KC):
                t = tmp_rot["n"] % 2
                tmp_rot["n"] += 1
                op("dve", lambda f=f, t=t: nc.vector.tensor_tensor(out=TMP[:, t, 0:wd], in0=XF[:, f, 0:wd],
                                                                   in1=PS[bank][:, 0:wd], op=ALU.mult),
                   reads=[("XF", f), ("PS", bank)], writes=[("TMP", t)])
                op("act", lambda f=f, t=t: nc.scalar.activation(out=H[:, f, 0:wd], in_=TMP[:, t, 0:wd],
                                                                func=AF.Identity, bias=SH(s3, f, r),
                                                                scale=GG(s3, f, r)),
                   reads=[("TMP", t)], writes=[("H", f)])

        def ffn(f_idx, s3, r, wd, tile_idx, bpos):
            sacc = None
            for half in range(2):
                if wd < 128:
                    pend = []

                    def flush_tm(item):
                        bi_, t_ = item
                        bk2 = mm_bank()
                        for h2 in range(2):
                            op("pe", lambda h2=h2: nc.tensor.transpose(
                                out=PS[bk2][:, h2 * 32:h2 * 32 + wd], in_=TMP[0:wd, t_, h2 * 128:(h2 + 1) * 128],
                                identity=IDENT[0:wd, 0:wd]),
                                reads=[("TMP", t_)], writes=[("PS", bk2)], signal=(h2 == 1))
                        j0 = bi_ * 2
                        op("act", lambda: nc.scalar.copy(
                            out=HID[:, j0 * 512:(j0 + 2) * 512].rearrange("p (a b) -> p a b", b=512)[:, :, 0:wd],
                            in_=PS[bk2][:, 0:64].rearrange("p (a b) -> p a b", b=32)[:, :, 0:wd]),
                            reads=[("PS", bk2)], writes=[("BA", j0), ("BA", j0 + 1)])

                    for bi in range(11):
                        s = load_block(tile_idx, bpos); bpos += 1
                        bank = mm_bank()
                        for k in range(KC):
                            op("pe", lambda s=s, k=k, bank=bank: nc.tensor.matmul(
                                PS[bank][0:wd, :], lhsT=H[:, k, 0:wd], rhs=WP[:, s, k * 512:(k + 1) * 512],
                                start=(k == 0), stop=(k == KC - 1)),
                                reads=[("WP", s), ("H", k)], writes=[("PS", bank)], signal=(k == KC - 1))
                        g = sg_rot["n"] % 2
                        sg_rot["n"] += 1
                        op("act", lambda g=g, bank=bank: nc.scalar.activation(
                            out=SG[0:wd, g, 0:256], in_=PS[bank][0:wd, 0:256], func=AF.Silu),
                            reads=[("PS", bank)], writes=[("SG", g)])
                        t = tmp_rot["n"] % 2
                        tmp_rot["n"] += 1
                        op("dve", lambda g=g, bank=bank, t=t: nc.vector.tensor_tensor(
                            out=TMP[0:wd, t, 0:256], in0=PS[bank][0:wd, 256:512], in1=SG[0:wd, g, 0:256],
                            op=ALU.mult),
                            reads=[("PS", bank), ("SG", g)], writes=[("TMP", t)])
                        if pend:
                            flush_tm(pend.pop(0))
                        pend.append((bi, t))
                    while pend:
                        flush_tm(pend.pop(0))
                for bi in range(11 if wd >= 128 else 0):
                    s = load_block(tile_idx, bpos); bpos += 1
                    for h2 in range(2):
                        j = bi * 2 + h2
                        bg = mm_bank()
                        bu = mm_bank()
                        for gu, bank in ((0, bg), (1, bu)):
                            for k in range(KC):
                                off = k * 512 + gu * 256 + h2 * 128
                                op("pe", lambda s=s, off=off, k=k, bank=bank: nc.tensor.matmul(
                                    PS[bank][:, 0:wd], lhsT=WP[:, s, off:off + 128], rhs=H[:, k, 0:wd],
                                    start=(k == 0), stop=(k == KC - 1)),
                                    reads=[("WP", s), ("H", k)], writes=[("PS", bank)], signal=(k == KC - 1))
                        g = sg_rot["n"] % 2
                        sg_rot["n"] += 1
                        op("act", lambda g=g, bg=bg: nc.scalar.activation(out=SG[:, g, 0:wd], in_=PS[bg][:, 0:wd],
                                                                          func=AF.Silu),
                           reads=[("PS", bg)], writes=[("SG", g)])
                        op("dve", lambda g=g, bu=bu, j=j: nc.vector.tensor_tensor(
                            out=HID[:, j * 512:j * 512 + wd], in0=PS[bu][:, 0:wd], in1=SG[:, g, 0:wd], op=ALU.mult),
                            reads=[("PS", bu), ("SG", g)], writes=[("BA", j)])
                if wd < 128:
                    pend_d = []

                    def flush_dn(item):
                        nonlocal sacc
                        mb_, t_ = item
                        bk2 = mm_bank()
                        for m2 in range(2):
                            op("pe", lambda m2=m2: nc.tensor.transpose(
                                out=PS[bk2][:, m2 * 32:m2 * 32 + wd], in_=TMP[0:wd, t_, m2 * 128:(m2 + 1) * 128],
                                identity=IDENT[0:wd, 0:wd]),
                                reads=[("TMP", t_)], writes=[("PS", bk2)], signal=(m2 == 1))
                        for m2 in range(2):
                            m = mb_ * 2 + m2
                            op("dve", lambda m=m, m2=m2: nc.vector.scalar_tensor_tensor(
                                out=XF[:, m, 0:wd], in0=PS[bk2][:, m2 * 32:m2 * 32 + wd], scalar=GT(s3, m, r),
                                in1=XF[:, m, 0:wd], op0=ALU.mult, op1=ALU.add),
                                reads=[("PS", bk2), ("XF", m)], writes=[("XF", m)])
                            if half == 1:
                                if sacc is None:
                                    sacc = StatAcc(wd, KC)
                                sacc.add(XF[:, m, 0:wd], [("XF", m)])

                    for mb in range(8):
                        s = load_block(tile_idx, bpos); bpos += 1
                        bank = mm_bank()
                        for k in range(22):
                            op("pe", lambda s=s, k=k, bank=bank: nc.tensor.matmul(
                                PS[bank][0:wd, 0:256], lhsT=HID[:, k * 512:k * 512 + wd],
                                rhs=WP[:, s, k * 256:(k + 1) * 256], start=(k == 0), stop=(k == 21)),
                                reads=[("WP", s), ("BA", k)], writes=[("PS", bank)], signal=(k == 21))
                        t = tmp_rot["n"] % 2
                        tmp_rot["n"] += 1
                        op("act", lambda t=t, bank=bank: nc.scalar.copy(out=TMP[0:wd, t, 0:256],
                                                                        in_=PS[bank][0:wd, 0:256]),
                           reads=[("PS", bank)], writes=[("TMP", t)])
                        if pend_d:
                            flush_dn(pend_d.pop(0))
                        pend_d.append((mb, t))
                    while pend_d:
                        flush_dn(pend_d.pop(0))
                for mb in range(8 if wd >= 128 else 0):
                    s = load_block(tile_idx, bpos); bpos += 1
                    for m2 in range(2):
                        m = mb * 2 + m2
                        bank = mm_bank()
                        for k in range(22):
                            off = k * 256 + m2 * 128
                            op("pe", lambda s=s, off=off, k=k, bank=bank: nc.tensor.matmul(
                                PS[bank][:, 0:wd], lhsT=WP[:, s, off:off + 128], rhs=HID[:, k * 512:k * 512 + wd],
                                start=(k == 0), stop=(k == 21)),
                                reads=[("WP", s), ("BA", k)], writes=[("PS", bank)], signal=(k == 21))
                        op("dve", lambda m=m, bank=bank: nc.vector.scalar_tensor_tensor(
                            out=XF[:, m, 0:wd], in0=PS[bank][:, 0:wd], scalar=GT(s3, m, r), in1=XF[:, m, 0:wd],
                            op0=ALU.mult, op1=ALU.add),
                            reads=[("PS", bank), ("XF", m)], writes=[("XF", m)])
                        if half == 1:
                            if sacc is None:
                                sacc = StatAcc(wd, KC)
                            sacc.add(XF[:, m, 0:wd], [("XF", m)])
            return bpos, sacc.finish()

        out_toks = []

        def tile(tile_idx, r, x_rows, y_rows, wd, first_in_seq, last_in_seq, is_sample, scp_dst):
            nrb = max(1, wd // 128)
            rows = min(wd, 128)
            bpos = 0
            use_pool = tile_idx >= pool_from
            for rb in range(nrb):
                hb = rb % 2
                sc.dma("sp", GLU[0:rows, 4 * hb:4 * hb + 4, 0:512],
                       x_rows[rb * rows:(rb + 1) * rows, :].rearrange("t (c f) -> t c f", c=4), tmi_sem[hb],
                       writes=[("GLU", 4 * hb + i) for i in range(4)])
                for q in range(4):
                    bank = mm_bank()
                    for f4 in range(4):
                        op("pe", lambda hb=hb, q=q, f4=f4, bank=bank: nc.tensor.transpose(
                            out=PS[bank][:, f4 * 128:f4 * 128 + rows],
                            in_=GLU[0:rows, 4 * hb + q, f4 * 128:(f4 + 1) * 128],
                            identity=IDENT[0:rows, 0:rows]),
                            reads=[("GLU", 4 * hb + q)], writes=[("PS", bank)], signal=(f4 == 3))
                    op("act", lambda q=q, bank=bank, rb=rb: nc.scalar.copy(
                        out=XF[:, q * 4:(q + 1) * 4, rb * 128:rb * 128 + rows],
                        in_=PS[bank][:, :].rearrange("p (a b) -> p a b", b=128)[:, :, 0:rows]),
                        reads=[("PS", bank)], writes=[("XF", q * 4 + i) for i in range(4)])
            norm_mod(0, r, wd)
            bpos, sbank = ffn(1, 0, r, wd, tile_idx, bpos)
            norm_mod(1, r, wd, bank=sbank)
            if first_in_seq:
                if is_sample:
                    sc.dma("sp", BIGA[0:HALO, 0:DB], cache, tmi_sem[0], writes=[("BA", i) for i in range(4)])
                    bank = mm_bank()
                    for c in range(8):
                        op("pe", lambda c=c, bank=bank: nc.tensor.transpose(
                            out=PS[bank][:, c * 32:c * 32 + HALO], in_=BIGA[0:HALO, c * 128:(c + 1) * 128],
                            identity=IDENT[0:HALO, 0:HALO]),
                            reads=[("BA", i) for i in range(4)], writes=[("PS", bank)], signal=(c == 7))
                    op("act", lambda bank=bank: nc.scalar.copy(
                        out=GLU[:, :, 0:HALO],
                        in_=PS[bank][:, 0:256].rearrange("p (a b) -> p a b", b=32)[:, :, 0:HALO]),
                        reads=[("PS", bank)], writes=[("GLU", c) for c in range(8)])
                else:
                    op("dve", lambda: nc.vector.memset(GLU[:, :, 0:HALO], 0.0),
                       writes=[("GLU", c) for c in range(8)])
            else:
                op("dve", lambda: nc.vector.tensor_copy(out=GLU[:, :, 0:HALO], in_=GLU[:, :, W:W + HALO]),
                   reads=[("GLU", c) for c in range(8)], writes=[("GLU", c) for c in range(8)])
            CVo = 4096

            def st_gab(nb):
                nonlocal bpos
                s = load_block(tile_idx, bpos); bpos += 1
                for h2 in range(2):
                    c = nb * 2 + h2
                    ba = mm_bank()
                    bb = mm_bank()
                    for gu, bank in ((0, ba), (1, bb)):
                        for k in range(KC):
                            off = k * 512 + gu * 256 + h2 * 128
                            op("pe", lambda s=s, off=off, k=k, bank=bank: nc.tensor.matmul(
                                PS[bank][:, 0:wd], lhsT=WP[:, s, off:off + 128], rhs=H[:, k, 0:wd],
                                start=(k == 0), stop=(k == KC - 1)),
                                reads=[("WP", s), ("H", k)], writes=[("PS", bank)], signal=(k == KC - 1))
                    g = sg_rot["n"] % 2
                    sg_rot["n"] += 1
                    op("act", lambda g=g, bb=bb: nc.scalar.activation(out=SG[:, g, 0:wd], in_=PS[bb][:, 0:wd],
                                                                      func=AF.Sigmoid),
                       reads=[("PS", bb)], writes=[("SG", g)])
                    if use_pool:
                        op("act", lambda ba=ba, c=c: nc.scalar.copy(out=GLU[:, c, HALO:HALO + wd],
                                                                    in_=PS[ba][:, 0:wd]),
                           reads=[("PS", ba)], writes=[("GLU", c)])
                        op("pool", lambda g=g, c=c: nc.gpsimd.tensor_tensor(
                            out=GLU[:, c, HALO:HALO + wd], in0=GLU[:, c, HALO:HALO + wd], in1=SG[:, g, 0:wd],
                            op=ALU.mult),
                            reads=[("GLU", c), ("SG", g)], writes=[("GLU", c)])
                    else:
                        op("dve", lambda g=g, ba=ba, c=c: nc.vector.tensor_tensor(
                            out=GLU[:, c, HALO:HALO + wd], in0=PS[ba][:, 0:wd], in1=SG[:, g, 0:wd], op=ALU.mult),
                            reads=[("PS", ba), ("SG", g)], writes=[("GLU", c)])
                    if c < 4:
                        op("act", lambda c=c: nc.scalar.copy(out=GLUB[:, c, 0:HALO + wd], in_=GLU[:, c, 0:HALO + wd]),
                           reads=[("GLU", c)], writes=[("GLUB", c)])

            def st_inu(ub):
                nonlocal bpos
                s = load_block(tile_idx, bpos); bpos += 1
                for m4 in range(4):
                    g = ub * 4 + m4
                    bank = mm_bank()
                    for k in range(KC):
                        off = k * 512 + m4 * 128
                        op("pe", lambda s=s, off=off, k=k, bank=bank: nc.tensor.matmul(
                            PS[bank][:, 0:wd], lhsT=WP[:, s, off:off + 128], rhs=H[:, k, 0:wd],
                            start=(k == 0), stop=(k == KC - 1)),
                            reads=[("WP", s), ("H", k)], writes=[("PS", bank)], signal=(k == KC - 1))
                    op("act", lambda g=g, bank=bank: nc.scalar.copy(out=UB[:, g, 0:wd], in_=PS[bank][:, 0:wd]),
                       reads=[("PS", bank)], writes=[("UB", g)])

            vt_box = {}

            def st_inv():
                nonlocal bpos
                sv = [load_block(tile_idx, bpos), load_block(tile_idx, bpos + 1)]
                bpos += 2
                for tb in range(nrb):
                    vbanks = []
                    for vb in range(2):
                        bank = mm_bank()
                        vbanks.append(bank)
                        s = sv[vb]
                        for k in range(KC):
                            op("pe", lambda s=s, k=k, bank=bank, tb=tb: nc.tensor.matmul(
                                PS[bank][0:rows, :], lhsT=H[:, k, tb * 128:tb * 128 + rows],
                                rhs=WP[:, s, k * 512:(k + 1) * 512], start=(k == 0), stop=(k == KC - 1)),
                                reads=[("WP", s), ("H", k)], writes=[("PS", bank)], signal=(k == KC - 1))
                    for vb in range(2):
                        q = sq_rot["n"] % 4
                        sq_rot["n"] += 1
                        op("act", lambda vb=vb, q=q, vbanks=vbanks: nc.scalar.activation(
                            out=SQ[0:rows, q, :], in_=PS[vbanks[vb]][0:rows, :], func=AF.Square,
                            accum_out=SS[0:rows, vb:vb + 1]),
                            reads=[("PS", vbanks[vb])], writes=[("SQ", q), ("SS",)])
                    if use_pool:
                        op("pool", lambda: nc.gpsimd.tensor_tensor(out=SS[0:rows, 2:3], in0=SS[0:rows, 0:1],
                                                                    in1=SS[0:rows, 1:2], op=ALU.add),
                           reads=[("SS",)], writes=[("SS2",)])
                        op("pool", lambda: nc.gpsimd.tensor_scalar(out=SS[0:rows, 3:4], in0=SS[0:rows, 2:3],
                                                                    scalar1=1.0 / DA, scalar2=EPS, op0=ALU.mult,
                                                                    op1=ALU.add),
                           reads=[("SS2",)], writes=[("SS3",)])
                        op("pool", lambda: nc.gpsimd.tensor_tensor(out=SS[0:rows, 4:5], in0=SS[0:rows, 3:4],
                                                                    in1=NEGH[0:rows, 0:1], op=ALU.pow),
                           reads=[("SS3",)], writes=[("SS4",)])
                        for vb in range(2):
                            g = sg_rot["n"] % 2
                            sg_rot["n"] += 1
                            op("act", lambda vb=vb, g=g, vbanks=vbanks: nc.scalar.activation(
                                out=SG[0:rows, g, :], in_=PS[vbanks[vb]][0:rows, :], func=AF.Identity,
                                scale=SS[0:rows, 4:5]),
                                reads=[("PS", vbanks[vb]), ("SS4",)], writes=[("SG", g)])
                            op("pool", lambda vb=vb, tb=tb, g=g: nc.gpsimd.tensor_tensor(
                                out=VN[0:rows, tb, vb * 512:(vb + 1) * 512], in0=SG[0:rows, g, :],
                                in1=GVB[0:rows, vb * 512:(vb + 1) * 512], op=ALU.mult),
                                reads=[("SG", g)], writes=[("VN", tb)])
                            if is_sample:
                                vt = 0
                                op("pool", lambda vb=vb, vt=vt, g=g: nc.gpsimd.tensor_tensor(
                                    out=BIGA[0:rows, DA + vb * 512:DA + (vb + 1) * 512], in0=SG[0:rows, g, :],
                                    in1=GVB[0:rows, vb * 512:(vb + 1) * 512], op=ALU.mult),
                                    reads=[("SG", g)], writes=[("BA", 4 + 2 * vb), ("BA", 5 + 2 * vb)])
                    else:
                        op("dve", lambda: nc.vector.tensor_tensor(out=SS[0:rows, 2:3], in0=SS[0:rows, 0:1],
                                                                  in1=SS[0:rows, 1:2], op=ALU.add),
                           reads=[("SS",)], writes=[("SS2",)])
                        op("act", lambda: nc.scalar.activation(out=SS[0:rows, 3:4], in_=SS[0:rows, 2:3],
                                                               func=AF.Sqrt, bias=EPS, scale=1.0 / DA),
                           reads=[("SS2",)], writes=[("SS3",)])
                        op("dve", lambda: nc.vector.reciprocal(out=SS[0:rows, 4:5], in_=SS[0:rows, 3:4]),
                           reads=[("SS3",)], writes=[("SS4",)])
                        for vb in range(2):
                            op("dve", lambda vb=vb, tb=tb, vbanks=vbanks: nc.vector.scalar_tensor_tensor(
                                out=VN[0:rows, tb, vb * 512:(vb + 1) * 512], in0=PS[vbanks[vb]][0:rows, :],
                                scalar=SS[0:rows, 4:5], in1=GVB[0:rows, vb * 512:(vb + 1) * 512],
                                op0=ALU.mult, op1=ALU.mult),
                                reads=[("PS", vbanks[vb]), ("SS4",)], writes=[("VN", tb)])
                            if is_sample:
                                vt = 0
                                op("dve", lambda vb=vb, vt=vt, vbanks=vbanks: nc.vector.scalar_tensor_tensor(
                                    out=BIGA[0:rows, DA + vb * 512:DA + (vb + 1) * 512], in0=PS[vbanks[vb]][0:rows, :],
                                    scalar=SS[0:rows, 4:5], in1=GVB[0:rows, vb * 512:(vb + 1) * 512],
                                    op0=ALU.mult, op1=ALU.mult),
                                    reads=[("PS", vbanks[vb]), ("SS4",)], writes=[("BA", 4 + 2 * vb), ("BA", 5 + 2 * vb)])
                if is_sample:
                    out_toks.append(sc.dma("pool", svs, BIGA[0:DEC, DA:2 * DA], tmo_sem[5],
                                           reads=[("BA", i) for i in range(4, 8)]))

            def st_cstate():
                if not last_in_seq:
                    return
                bank = mm_bank()
                bank2 = mm_bank()
                for c in range(8):
                    bk = bank if c < 4 else bank2
                    op("pe", lambda c=c, bk=bk: nc.tensor.transpose(
                        out=PS[bk][0:HALO, (c % 4) * 128:(c % 4 + 1) * 128], in_=GLU[:, c, wd:wd + HALO],
                        identity=IDENT[:, :]),
                        reads=[("GLU", c)], writes=[("PS", bk)], signal=(c % 4 == 3))
                op("act", lambda: nc.scalar.copy(out=BIGA[0:HALO, 0:512], in_=PS[bank][0:HALO, :]),
                   reads=[("PS", bank)], writes=[("BA", 0), ("BA", 1)])
                op("act", lambda: nc.scalar.copy(out=BIGA[0:HALO, 512:1024], in_=PS[bank2][0:HALO, :]),
                   reads=[("PS", bank2)], writes=[("BA", 2), ("BA", 3)])
                out_toks.append(sc.dma("pool", scp_dst, BIGA[0:HALO, 0:DB], tmo_sem[4],
                                       reads=[("BA", i) for i in range(4)]))

            def st_mixa():
                for g in range(NG):
                    bank = mm_bank()
                    for tb in range(nrb):
                        op("pe", lambda g=g, tb=tb, bank=bank: nc.tensor.matmul(
                            PS[bank][:, tb * 128:tb * 128 + rows], lhsT=VN[0:rows, tb, g * 128:(g + 1) * 128],
                            rhs=WST[0:rows, g, 0:rows], start=True, stop=False),
                            reads=[("VN", tb)], writes=[("PS", bank)], signal=False)
                        op("pe", lambda g=g, tb=tb, bank=bank: nc.tensor.matmul(
                            PS[bank][:, tb * 128:tb * 128 + rows], lhsT=ONES[:, :],
                            rhs=BS128[:, g * 128:g * 128 + rows], start=False, stop=True),
                            reads=[], writes=[("PS", bank)], signal=(tb == nrb - 1))
                    op("dve", lambda g=g, bank=bank: nc.vector.tensor_tensor(
                        out=BIGA[:, g * 512:g * 512 + wd], in0=PS[bank][:, 0:wd], in1=UB[:, g, 0:wd],
                        op=ALU.mult),
                        reads=[("PS", bank), ("UB", g)], writes=[("BA", 2 * g), ("BA", 2 * g + 1)])

            conv_cb = {}

            dg_rot = {"n": 0}

            def conv_mm(c):
                bank = mm_bank()
                conv_cb[c] = bank
                for k in range(CW):
                    i = dg_rot["n"] % NDG
                    dg_rot["n"] += 1
                    wk = WDW[:, c * CW + k:c * CW + k + 1]
                    if use_pool:
                        op("pool", lambda: nc.gpsimd.tensor_scalar(out=DG[:, i, :], in0=IDB[:, :], scalar1=wk,
                                                                    scalar2=0.0, op0=ALU.mult, op1=ALU.add),
                           writes=[("DG", i)])
                    else:
                        op("act", lambda: nc.scalar.activation(out=DG[:, i, :], in_=IDB[:, :], func=AF.Identity,
                                                               scale=wk),
                           writes=[("DG", i)])
                    op("pe", lambda: nc.tensor.matmul(PS[bank][:, 0:wd], lhsT=DG[:, i, :], rhs=GLUB[:, c, k:k + wd],
                                                      start=(k == 0), stop=(k == CW - 1)),
                       reads=[("DG", i), ("GLUB", c)], writes=[("PS", bank)], signal=True)

            def conv_evac(c):
                cb = conv_cb[c]
                bdw = G8[:, c:c + 1]
                op("act", lambda: nc.scalar.activation(out=BIGA[:, CVo + c * 512:CVo + c * 512 + wd],
                                                       in_=PS[cb][:, 0:wd], func=AF.Identity, bias=bdw, scale=1.0),
                   reads=[("PS", cb)], writes=[("BA", 16 + 2 * c), ("BA", 17 + 2 * c)])
                x2 = c % 2
                op("act", lambda: nc.scalar.activation(out=XB[:, x2, 0:wd], in_=PS[cb][:, 0:wd], func=AF.Identity,
                                                       bias=bdw, scale=1.0),
                   reads=[("PS", cb)], writes=[("XB", x2)])
                q = sq_rot["n"] % 4
                sq_rot["n"] += 1
                conv_sq[c] = q
                op("act", lambda: nc.scalar.activation(out=SQ[:, q, 0:wd], in_=PS[cb][:, 0:wd], func=AF.Square,
                                                       bias=bdw, scale=1.0),
                   reads=[("PS", cb)], writes=[("SQ", q)])

            def conv_taps2(c0, c1):
                cbs = {}
                for c in (c0, c1):
                    cbs[c] = 6 + (cvb_rot["n"] % 2)
                    cvb_rot["n"] += 1
                    conv_cb[c] = cbs[c]
                for k in range(CW):
                    for c in (c0, c1):
                        cb = cbs[c]
                        if k == 0:
                            op("dve", lambda: nc.vector.tensor_scalar(
                                out=PS[cb][:, 0:wd], in0=GLU[:, c, 0:wd], scalar1=WDW[:, c * CW:c * CW + 1],
                                scalar2=G8[:, c:c + 1], op0=ALU.mult, op1=ALU.add),
                                reads=[("GLU", c)], writes=[("PS", cb)])
                        else:
                            op("dve", lambda: nc.vector.scalar_tensor_tensor(
                                out=PS[cb][:, 0:wd], in0=GLU[:, c, k:k + wd],
                                scalar=WDW[:, c * CW + k:c * CW + k + 1],
                                in1=PS[cb][:, 0:wd], op0=ALU.mult, op1=ALU.add),
                                reads=[("GLU", c), ("PS", cb)], writes=[("PS", cb)])

            def conv_evac_plain(c):
                cb = conv_cb[c]
                op("act", lambda: nc.scalar.copy(out=BIGA[:, CVo + c * 512:CVo + c * 512 + wd], in_=PS[cb][:, 0:wd]),
                   reads=[("PS", cb)], writes=[("BA", 16 + 2 * c), ("BA", 17 + 2 * c)])
                x2 = c % 2
                op("act", lambda: nc.scalar.copy(out=XB[:, x2, 0:wd], in_=PS[cb][:, 0:wd]),
                   reads=[("PS", cb)], writes=[("XB", x2)])
                q = sq_rot["n"] % 4
                sq_rot["n"] += 1
                conv_sq[c] = q
                op("act", lambda: nc.scalar.activation(out=SQ[:, q, 0:wd], in_=PS[cb][:, 0:wd], func=AF.Square),
                   reads=[("PS", cb)], writes=[("SQ", q)])

            n_stats = {"n": 0}

            conv_sq = {}

            def conv_stats(c):
                x2 = c % 2
                q = conv_sq[c]
                first = n_stats["n"] == 0
                last = n_stats["n"] == 7
                n_stats["n"] += 1
                op("pe", lambda: nc.tensor.matmul(PS[4][:, 0:wd], lhsT=ONES[:], rhs=XB[:, x2, 0:wd],
                                                  start=first, stop=last),
                   reads=[("XB", x2)], writes=[("PS", 4)], signal=True)
                op("pe", lambda: nc.tensor.matmul(PS[5][:, 0:wd], lhsT=ONES[:], rhs=SQ[:, q, 0:wd],
                                                  start=first, stop=last),
                   reads=[("SQ", q)], writes=[("PS", 5)], signal=True)

            def ln_tail():
                t0 = tmp_rot["n"] % 2
                tmp_rot["n"] += 1
                op("act", lambda: nc.scalar.activation(out=TMP[:, t0, 0:wd], in_=PS[4][:, 0:wd], func=AF.Square,
                                                       scale=1.0 / DB),
                   reads=[("PS", 4)], writes=[("TMP", t0)])
                op("act", lambda: nc.scalar.mul(out=PS[4][:, 0:wd], in_=PS[4][:, 0:wd], mul=1.0 / DB),
                   reads=[("PS", 4)], writes=[("PS", 4)])
                op("dve", lambda: nc.vector.scalar_tensor_tensor(out=PS[5][:, 0:wd], in0=PS[5][:, 0:wd], scalar=1.0 / DB,
                                                                 in1=TMP[:, t0, 0:wd], op0=ALU.mult, op1=ALU.subtract),
                   reads=[("PS", 5), ("TMP", t0)], writes=[("PS", 5)])
                op("act", lambda: nc.scalar.activation(out=PS[5][:, 0:wd], in_=PS[5][:, 0:wd], func=AF.Sqrt,
                                                       bias=EPS, scale=1.0),
                   reads=[("PS", 5)], writes=[("PS", 5)])
                op("dve", lambda: nc.vector.reciprocal(out=PS[5][:, 0:wd], in_=PS[5][:, 0:wd]),
                   reads=[("PS", 5)], writes=[("PS", 5)])
                ybs = []
                for c in range(8):
                    cvr = [("BA", 16 + 2 * c), ("BA", 17 + 2 * c)]
                    cv = BIGA[:, CVo + c * 512:CVo + c * 512 + wd]
                    t1 = tmp_rot["n"] % 2
                    tmp_rot["n"] += 1
                    op("dve", lambda cv=cv, t1=t1: nc.vector.tensor_tensor(out=TMP[:, t1, 0:wd], in0=cv,
                                                                           in1=PS[4][:, 0:wd], op=ALU.subtract),
                       reads=cvr + [("PS", 4)], writes=[("TMP", t1)])
                    op("dve", lambda t1=t1: nc.vector.tensor_tensor(out=TMP[:, t1, 0:wd], in0=TMP[:, t1, 0:wd],
                                                                    in1=PS[5][:, 0:wd], op=ALU.mult),
                       reads=[("TMP", t1), ("PS", 5)], writes=[("TMP", t1)])
                    op("act", lambda cv=cv, t1=t1, c=c: nc.scalar.activation(
                        out=cv, in_=TMP[:, t1, 0:wd], func=AF.Silu, bias=G8[:, 16 + c:17 + c], scale=G8[:, 8 + c:9 + c]),
                        reads=[("TMP", t1)], writes=cvr)
                    ybs.append((cv, cvr))
                return ybs

            def ln_tail2(ybs):
                sumsq(ybs, wd, bank=4)
                rstd_from_sum(4, wd, DB)
                for c in range(8):
                    cv, cvr = ybs[c]
                    op("dve", lambda c=c, cv=cv: nc.vector.scalar_tensor_tensor(
                        out=H[:, 8 + c, 0:wd], in0=cv, scalar=G8[:, 32 + c:33 + c], in1=PS[4][:, 0:wd],
                        op0=ALU.mult, op1=ALU.mult),
                        reads=cvr + [("PS", 4)], writes=[("H", 8 + c)])

            st_gab(2)
            conv_taps2(4, 5)
            st_gab(3)
            st_gab(0)
            conv_evac_plain(4)
            conv_evac_plain(5)
            conv_taps2(6, 7)
            st_gab(1)
            conv_stats(4)
            conv_stats(5)
            conv_mm(0); conv_evac(0)
            conv_mm(1); conv_evac(1); conv_stats(0)
            conv_mm(2); conv_evac(2); conv_stats(1)
            conv_mm(3); conv_evac(3); conv_stats(2)
            st_inu(0)
            conv_stats(3)
            conv_evac_plain(6); conv_stats(6)
            conv_evac_plain(7); conv_stats(7)
            ybs_ = ln_tail()
            st_inu(1)
            st_inv()
            ln_tail2(ybs_)
            st_cstate()
            st_mixa()
            bank = sumsq([(BIGA[:, g * 512:g * 512 + wd], [("BA", 2 * g), ("BA", 2 * g + 1)]) for g in range(NG)], wd,
                         bank=5)
            rstd_from_sum(bank, wd, DA)
            for g in range(NG):
                op("dve", lambda g=g, bank=bank: nc.vector.scalar_tensor_tensor(
                    out=H[:, g, 0:wd], in0=BIGA[:, g * 512:g * 512 + wd], scalar=G8[:, 24 + g:25 + g],
                    in1=PS[bank][:, 0:wd], op0=ALU.mult, op1=ALU.mult),
                    reads=[("BA", 2 * g), ("BA", 2 * g + 1), ("PS", bank)], writes=[("H", g)])
            sacc3 = None
            for ob in range(4):
                s = load_block(tile_idx, bpos); bpos += 1
                for m4 in range(4):
                    m = ob * 4 + m4
                    bank = mm_bank()
                    for k in range(KC):
                        off = k * 512 + m4 * 128
                        op("pe", lambda s=s, off=off, k=k, bank=bank: nc.tensor.matmul(
                            PS[bank][:, 0:wd], lhsT=WP[:, s, off:off + 128], rhs=H[:, k, 0:wd],
                            start=(k == 0), stop=(k == KC - 1)),
                            reads=[("WP", s), ("H", k)], writes=[("PS", bank)], signal=(k == KC - 1))
                    op("dve", lambda m=m, bank=bank: nc.vector.scalar_tensor_tensor(
                        out=XF[:, m, 0:wd], in0=PS[bank][:, 0:wd], scalar=GT(1, m, r), in1=XF[:, m, 0:wd],
                        op0=ALU.mult, op1=ALU.add),
                        reads=[("PS", bank), ("XF", m)], writes=[("XF", m)])
                    if sacc3 is None:
                        sacc3 = StatAcc(wd, KC)
                    sacc3.add(XF[:, m, 0:wd], [("XF", m)])
            norm_mod(2, r, wd, bank=sacc3.finish())
            bpos, bank = ffn(2, 2, r, wd, tile_idx, bpos)
            assert bpos == NBLK
            rstd_from_sum(bank, wd, D)
            for f in range(KC):
                op("dve", lambda f=f, bank=bank: nc.vector.scalar_tensor_tensor(
                    out=XF[:, f, 0:wd], in0=XF[:, f, 0:wd], scalar=GD[:, 3 * KC + f:3 * KC + f + 1],
                    in1=PS[bank][:, 0:wd], op0=ALU.mult, op1=ALU.mult),
                    reads=[("XF", f), ("PS", bank)], writes=[("XF", f)])
            for rb in range(nrb):
                for q in range(4):
                    bk = mm_bank()
                    for f4 in range(4):
                        f = q * 4 + f4
                        op("pe", lambda f=f, f4=f4, bk=bk, rb=rb: nc.tensor.transpose(
                            out=PS[bk][0:rows, f4 * 128:(f4 + 1) * 128], in_=XF[:, f, rb * 128:rb * 128 + rows],
                            identity=IDENT[:, :]),
                            reads=[("XF", f)], writes=[("PS", bk)], signal=(f4 == 3))
                    o0 = rb * D + q * 512
                    regs = [("BA", rb * 8 + q * 2), ("BA", rb * 8 + q * 2 + 1)]
                    if q % 2 == 0:
                        op("act", lambda o0=o0, bk=bk: nc.scalar.copy(out=BIGA[0:rows, o0:o0 + 512],
                                                                      in_=PS[bk][0:rows, :]),
                           reads=[("PS", bk)], writes=regs)
                    else:
                        op("dve", lambda o0=o0, bk=bk: nc.vector.tensor_copy(out=BIGA[0:rows, o0:o0 + 512],
                                                                             in_=PS[bk][0:rows, :]),
                           reads=[("PS", bk)], writes=regs)
                out_toks.append(sc.dma("pool", y_rows[rb * rows:(rb + 1) * rows, :], BIGA[0:rows, rb * D:(rb + 1) * D],
                                       tmo_sem[rb], reads=[("BA", rb * 8 + i) for i in range(8)]))

        tidx = 0
        for b in range(2):
            for tt in range(NTB):
                tile(tidx, b, xp[b, tt * W:(tt + 1) * W, :], yp[b, tt * W:(tt + 1) * W, :], W,
                     first_in_seq=(tt == 0), last_in_seq=(tt == NTB - 1), is_sample=False, scp_dst=scp[b])
                tidx += 1
        tile(tidx, 2, xs, ys, DEC, first_in_seq=True, last_in_seq=True, is_sample=True, scp_dst=scs)

        final = {}
        for s, v in out_toks:
            final[s] = max(final.get(s, 0), v)
        for s, v in final.items():
            nc.gpsimd.wait_ge(sc.sems[s], v)
        nc.sync.wait_ge(sc.sems["pe"], sc.val["pe"])
        nc.scalar.wait_ge(sc.sems["pe"], sc.val["pe"])
        build.stats = dict(n_wait=sc.n_wait, vals=dict(sc.val))
    return nc


def _fm(v, nchunk):
    return np.ascontiguousarray(np.asarray(v, np.float32).reshape(nchunk, 128).T)


def make_in_maps(inp, S=2048, n_cores=N_CORES):
    f32 = lambda a: np.ascontiguousarray(np.asarray(a, dtype=np.float32))
    x_prompt = f32(inp["x_prompt"]); x_sample = f32(inp["x_sample"])
    cache = f32(inp["cache_conv"])[0]
    c_prompt = f32(inp["c_prompt"]); c_sample = f32(inp["c_sample"])
    w_ada = f32(inp["w_ada"])[0]
    b_ada = _fm(f32(inp["b_ada"])[0], NMOD * KC)
    gd = np.concatenate([_fm(f32(inp[k])[0] if k != "g_final" else f32(inp[k]), KC)
                         for k in ("g_ffn1", "g_mix", "g_ffn2", "g_final")], axis=1)
    g8 = np.concatenate([_fm(f32(inp[k])[0], 8) for k in ("b_dw", "g_cn", "b_cn", "g_out_a", "g_out_b")], axis=1)
    wdw = f32(inp["w_dw"])[0]
    wdw_fm = np.ascontiguousarray(wdw.reshape(CW, 8, 128).transpose(2, 1, 0).reshape(128, 8 * CW))
    gvb = np.ascontiguousarray(np.broadcast_to(f32(inp["g_v"])[0][None, :], (128, DA)))
    ws = f32(inp["w_s"])[0]
    wst = np.ascontiguousarray(ws.transpose(2, 0, 1).reshape(128, NG * 128))
    trilT = np.ascontiguousarray(np.triu(np.ones((128, 128), np.float32)))
    bsv = np.ascontiguousarray(f32(inp["b_s"])[0].reshape(1, NG * 128))
    ident = np.eye(128, dtype=np.float32)
    shared = dict(w_ada=w_ada, b_ada=b_ada, gd=np.ascontiguousarray(gd), g8=np.ascontiguousarray(g8), wdw=wdw_fm,
                  gvb=gvb, wst=wst, trilT=trilT, bs=bsv, ident=ident,
                  w_up1=f32(inp["w_up1"])[0], w_up2=f32(inp["w_up2"])[0],
                  w_down1=f32(inp["w_down1"])[0], w_down2=f32(inp["w_down2"])[0],
                  w_in=f32(inp["w_in"])[0], w_out=f32(inp["w_out"])[0])
    maps = []
    for c in range(n_cores):
        c3 = np.stack([c_prompt[2 * c], c_prompt[2 * c + 1], c_sample[c]], axis=0)
        c3t = np.ascontiguousarray(c3.reshape(3, KC, 128).transpose(2, 1, 0).reshape(128, KC * 3))
        m = dict(shared)
        m.update(xp=np.ascontiguousarray(x_prompt[2 * c:2 * c + 2, :S]), xs=np.ascontiguousarray(x_sample[c]),
                 cache=np.ascontiguousarray(cache[c]), c3t=c3t)
        maps.append(m)
    return maps


def kernel(**inputs):
    S = 2048
    nc = build(S)
    maps = make_in_maps(inputs, S)
    res = run_bass_kernel_spmd(nc, maps, core_ids=list(range(N_CORES)))
    rs = res.results
    y_prompt = np.concatenate([r["yp"] for r in rs], axis=0).astype(np.float32)
    y_sample = np.stack([r["ys"] for r in rs], axis=0).astype(np.float32)
    scp = np.concatenate([r["scp"] for r in rs], axis=0)[None].astype(np.float32)
    scs = np.stack([r["scs"] for r in rs], axis=0)[None].astype(np.float32)
    svs = np.stack([r["svs"] for r in rs], axis=0)[None].astype(np.float32)
    return (y_prompt, y_sample, scp, scs, svs)
```

```python
import numpy as np
from contextlib import ExitStack
import concourse.bass as bass
import concourse.mybir as mybir
from concourse.bass_utils import run_bass_kernel_spmd

F32 = mybir.dt.float32
BF16 = mybir.dt.bfloat16
AF = mybir.ActivationFunctionType
ALU = mybir.AluOpType

D = 2048
KC = 16
DFF = 5632
DA = 1024
DB = 1024
NG = 8
CW = 31
HALO = 30
EPS = 1e-6
W = 512
DEC = 32
NSLOT = 4
SLOT = 8192
NTM = 2
NMOD = 9
N_CORES = 8
POOL_FROM_TILE = 2


class Sched:
    def __init__(self, nc, es):
        self.nc = nc
        self.E = {"pe": nc.tensor, "act": nc.scalar, "dve": nc.vector, "pool": nc.gpsimd, "sp": nc.sync}
        self.sems = {}
        self.val = {}
        for e in ("pe", "act", "dve", "pool"):
            self.sems[e] = es.enter_context(nc.semaphore("sem_" + e))
            self.val[e] = 0
        self.es = es
        self.known = {e: {} for e in self.E}
        self.lastw = {}
        self.rd = {}
        self.n_wait = 0

    def new_sem(self, name):
        self.sems[name] = self.es.enter_context(self.nc.semaphore(name))
        self.val[name] = 0
        return name

    def _deps(self, reads, writes):
        need = {}

        def add(tok):
            s, v = tok
            if need.get(s, 0) < v:
                need[s] = v

        for r in reads:
            t = self.lastw.get(r)
            if t:
                add(t)
        for w in writes:
            t = self.lastw.get(w)
            if t:
                add(t)
            for s, v in self.rd.get(w, {}).items():
                add((s, v))
        return need

    def _record(self, tok, reads, writes):
        s, v = tok
        for r in reads:
            d = self.rd.setdefault(r, {})
            if d.get(s, 0) < v:
                d[s] = v
        for w in writes:
            self.lastw[w] = tok
            self.rd[w] = {}

    def _emit_waits(self, e, need, embed_ok):
        k = self.known[e]
        todo = []
        for s, v in need.items():
            if e == "pe" and s == "pe":
                continue
            if k.get(s, 0) >= v:
                continue
            todo.append((s, v))
            k[s] = v
        emb = None
        if embed_ok and todo:
            emb = todo.pop()
        for s, v in todo:
            self.E[e].wait_ge(self.sems[s], v)
            self.n_wait += 1
        return emb

    def op(self, e, fn, reads=(), writes=(), signal=True, extra=None):
        need = self._deps(reads, writes)
        if extra:
            for s, v in extra:
                if need.get(s, 0) < v:
                    need[s] = v
        emb = self._emit_waits(e, need, embed_ok=(e != "pe"))
        ins = fn()
        if emb is not None:
            ins.wait_op(self.sems[emb[0]], emb[1], "sem-ge")
        if signal:
            ins.then_inc(self.sems[e], 1)
            self.val[e] += 1
            tok = (e, self.val[e])
        else:
            tok = (e, self.val[e] + 1)
        self._record(tok, reads, writes)
        return tok

    def dma(self, q, out, in_, sem, reads=(), writes=(), extra=None):
        need = self._deps(reads, writes)
        if extra:
            for s, v in extra:
                if need.get(s, 0) < v:
                    need[s] = v
        self._emit_waits(q, need, embed_ok=False)
        ins = self.E[q].dma_start(out=out, in_=in_)
        ins.then_inc(self.sems[sem], 16)
        self.val[sem] += 16
        tok = (sem, self.val[sem])
        self._record(tok, reads, writes)
        return tok

    def wait_tok(self, e, toks):
        need = {}
        for s, v in toks:
            if need.get(s, 0) < v:
                need[s] = v
        self._emit_waits(e, need, embed_ok=False)


def block_list():
    bl = []
    for f in (1, 2):
        ffn = []
        for half in range(2):
            for bi in range(11):
                ffn.append(("UP", f, half * 11 + bi))
            for mb in range(8):
                ffn.append(("DN", f, half, mb))
        if f == 1:
            bl += ffn
            for nb in (2, 3, 0, 1):
                bl.append(("GAB", nb))
            for ub in range(2):
                bl.append(("INU", ub))
            for vb in range(2):
                bl.append(("INV", vb))
            for ob in range(4):
                bl.append(("OUT", ob))
        else:
            bl += ffn
    return bl


def build(S=2048, n_conv_sems=8, pool_from=POOL_FROM_TILE):
    NTB = S // W
    nc = bass.Bass("TRN2", target_bir_lowering=False, dynamic_dma_scratch_size=8192)
    dt = nc.dram_tensor
    xp = dt("xp", [2, S, D], F32, kind="ExternalInput").ap()
    xs = dt("xs", [DEC, D], F32, kind="ExternalInput").ap()
    cache = dt("cache", [HALO, DB], F32, kind="ExternalInput").ap()
    c3t = dt("c3t", [128, KC * 3], F32, kind="ExternalInput").ap()
    w_ada = dt("w_ada", [D, NMOD * D], F32, kind="ExternalInput").ap()
    b_ada = dt("b_ada", [128, NMOD * KC], F32, kind="ExternalInput").ap()
    gd = dt("gd", [128, 4 * KC], F32, kind="ExternalInput").ap()
    g8 = dt("g8", [128, 5 * 8], F32, kind="ExternalInput").ap()
    wdw = dt("wdw", [128, 8 * CW], F32, kind="ExternalInput").ap()
    gvb = dt("gvb", [128, DA], F32, kind="ExternalInput").ap()
    wst = dt("wst", [128, NG * 128], F32, kind="ExternalInput").ap()
    trilT = dt("trilT", [128, 128], F32, kind="ExternalInput").ap()
    bs = dt("bs", [1, NG * 128], F32, kind="ExternalInput").ap()
    ident_d = dt("ident", [128, 128], F32, kind="ExternalInput").ap()
    w_up = {1: dt("w_up1", [D, 2 * DFF], F32, kind="ExternalInput").ap(),
            2: dt("w_up2", [D, 2 * DFF], F32, kind="ExternalInput").ap()}
    w_dn = {1: dt("w_down1", [DFF, D], F32, kind="ExternalInput").ap(),
            2: dt("w_down2", [DFF, D], F32, kind="ExternalInput").ap()}
    w_in = dt("w_in", [D, 4 * DA], F32, kind="ExternalInput").ap()
    w_out = dt("w_out", [D, D], F32, kind="ExternalInput").ap()
    yp = dt("yp", [2, S, D], F32, kind="ExternalOutput").ap()
    ys = dt("ys", [DEC, D], F32, kind="ExternalOutput").ap()
    scp = dt("scp", [2, HALO, DB], F32, kind="ExternalOutput").ap()
    scs = dt("scs", [HALO, DB], F32, kind="ExternalOutput").ap()
    svs = dt("svs", [DEC, DA], F32, kind="ExternalOutput").ap()

    blocks = block_list()
    NBLK = len(blocks)
    scr = dt("scr", [NBLK, 128, SLOT], BF16, kind="Internal").ap()

    def conv_srcs(b, base=None):
        blk = blocks[b]
        if base is None:
            base = scr[b]
        kind = blk[0]
        out = []
        if kind == "UP":
            _, f, bi = blk
            src = w_up[f].rearrange("(k p) (gu j c) -> p k gu j c", p=128, gu=2, c=256)
            dst = base.rearrange("p (k gu c) -> p k gu c", k=KC, gu=2)
            for gu in range(2):
                out.append((dst[:, :, gu, :], src[:, :, gu, bi, :]))
        elif kind == "DN":
            _, f, half, mb = blk
            src = w_dn[f].rearrange("(hh k p) (mb c) -> p hh k mb c", p=128, k=22, c=256)
            dst = base[:, 0:22 * 256].rearrange("p (k c) -> p k c", c=256)
            out.append((dst, src[:, half, :, mb, :]))
        elif kind == "GAB":
            _, nb = blk
            src = w_in.rearrange("(k p) (sec j c) -> p k sec j c", p=128, sec=4, c=256)
            dst = base.rearrange("p (k gu c) -> p k gu c", k=KC, gu=2)
            for gu in range(2):
                out.append((dst[:, :, gu, :], src[:, :, 2 + gu, nb, :]))
        elif kind in ("INU", "INV"):
            j = blk[1] + (0 if kind == "INU" else 2)
            src = w_in.rearrange("(k p) (j c) -> p k j c", p=128, c=512)
            dst = base.rearrange("p (k c) -> p k c", c=512)
            out.append((dst, src[:, :, j, :]))
        elif kind == "OUT":
            src = w_out.rearrange("(k p) (j c) -> p k j c", p=128, c=512)
            dst = base.rearrange("p (k c) -> p k c", c=512)
            out.append((dst, src[:, :, blk[1], :]))
        return out

    with ExitStack() as es:
        sb = lambda name, shape, dtype: es.enter_context(nc.sbuf_tensor(name, shape, dtype))
        XF = sb("XF", [128, KC, W], F32)
        H = sb("H", [128, KC, W], BF16)
        BIGA = sb("BIGA", [128, 8192], F32)
        HID = BIGA.bitcast(BF16)
        UB = sb("UB", [128, 8, W], BF16)
        VN = sb("VN", [128, 4, DA], BF16)
        GLU = sb("GLU", [128, 8, HALO + W], F32)
        WP = sb("WP", [128, NSLOT, SLOT], BF16)
        SQ = sb("SQ", [128, 4, W], BF16)
        TMP = sb("TMP", [128, 2, W], F32)
        SG = sb("SG", [128, 2, W], F32)
        XB = sb("XB", [128, 2, W], BF16)
        GLUB = sb("GLUB", [128, 4, HALO + W], BF16)
        NDG = 4
        DG = sb("DG", [128, NDG, 128], BF16)
        IDB = sb("IDB", [128, 128], BF16)
        MOD = TMP[:, 0, 0:NMOD * KC * 3].rearrange("p (c r) -> p c r", r=3)
        GS = sb("GS", [128, NMOD * KC, 3], F32)
        GD = sb("GD", [128, 4 * KC], F32)
        G8 = sb("G8", [128, 5 * 8], F32)
        WDW = sb("WDW", [128, 8 * CW], F32)
        GVB = sb("GVB", [128, DA], F32)
        WST = sb("WST", [128, NG, 128], BF16)
        TRI = SG[:, 0, 0:128]
        BS128 = sb("BS128", [128, NG * 128], BF16)
        IDENT = sb("IDENT", [128, 128], F32)
        ONES = sb("ONES", [128, 128], BF16)
        C3 = sb("C3", [128, KC * 3], F32)
        C3B = sb("C3B", [128, KC * 3], BF16)
        BADA = sb("BADA", [128, NMOD * KC], F32)
        SS = sb("SS", [128, 8], F32)
        NEGH = sb("NEGH", [128, 1], F32)
        PS = [es.enter_context(nc.psum_tensor("ps%d" % i, [128, W], F32)) for i in range(8)]

        sc = Sched(nc, es)
        op = sc.op
        slot_sem = [sc.new_sem("s_slot%d" % i) for i in range(NSLOT)]
        slot_sem_sw = [sc.new_sem("s_slotsw%d" % i) for i in range(NSLOT)]
        tmi_sem = [sc.new_sem("s_tmi%d" % i) for i in range(4)]
        tmo_sem = [sc.new_sem("s_tmo%d" % i) for i in range(6)]
        cst_sem = sc.new_sem("s_cst")
        store_sem = [sc.new_sem("s_store%d" % i) for i in range(NSLOT)]
        misc_sem = sc.new_sem("s_misc")

        cst_toks = []
        for dst, src in ((C3, c3t), (BADA, b_ada), (GD, gd), (G8, g8), (WDW, wdw), (GVB, gvb),
                         (IDENT, ident_d)):
            cst_toks.append(sc.dma("sp", dst[:], src, cst_sem))
        cst_toks.append(sc.dma("sp", TRI, trilT, cst_sem, writes=[("SG", 0)]))
        BSF = TMP[0:1, :, :].rearrange("p a b -> p (a b)")
        cst_toks.append(sc.dma("sp", BSF, bs, cst_sem, writes=[("TMP", 0), ("TMP", 1)]))
        cst_toks.append(sc.dma("sp", BIGA[:, 0:NG * 128], wst, cst_sem, writes=[("BA", i) for i in range(4)]))
        cst_all = [cst_toks[-1]]
        st = []
        st.append(op("dve", lambda: nc.vector.memset(ONES[:], 1.0), extra=cst_all, writes=[("ONES",)]))
        st.append(op("dve", lambda: nc.vector.memset(BS128[:], 0.0), writes=[("BS128",)]))
        st.append(op("dve", lambda: nc.vector.memset(NEGH[:], -0.5), writes=[("NEGH",)]))
        st.append(op("dve", lambda: nc.vector.tensor_copy(out=IDB[:], in_=IDENT[:]), writes=[("IDB",)]))
        st.append(op("dve", lambda: nc.vector.tensor_copy(out=BS128[0:1, :], in_=BSF),
                     reads=[("TMP", 0), ("TMP", 1)], writes=[("BS128",)]))
        for g in range(NG):
            st.append(op("dve", lambda g=g: nc.vector.tensor_tensor(
                out=WST[:, g, :], in0=BIGA[:, g * 128:(g + 1) * 128], in1=TRI, op=ALU.mult),
                reads=[("BA", i) for i in range(4)] + [("SG", 0)], writes=[("WST", g)]))
        st.append(op("act", lambda: nc.scalar.activation(out=C3B[:], in_=C3[:], func=AF.Silu), extra=cst_all,
                     writes=[("C3B",)]))

        ring = {"n": 0}
        pe_ring = {"n": 0}

        def slot_of(seq):
            return seq % NSLOT

        NADA = (NMOD * D) // 512
        wa_v = w_ada.rearrange("(k p) (j c) -> p k j c", p=128, c=512)
        ada_tok = None
        for j in range(NADA):
            seq = ring["n"]; ring["n"] += 1
            s = slot_of(seq)
            sc.dma("pool", WP[:, s, :].rearrange("p (k c) -> p k c", c=512), wa_v[:, :, j, :],
                   slot_sem_sw[s], writes=[("WP", s)])
            for m4 in range(4):
                col = j * 4 + m4
                for k in range(KC):
                    ada_tok = op("pe", lambda s=s, k=k, m4=m4, col=col: nc.tensor.matmul(
                        PS[0][:, col * 3:(col + 1) * 3],
                        lhsT=WP[:, s, k * 512 + m4 * 128:k * 512 + (m4 + 1) * 128],
                        rhs=C3B[:, k * 3:(k + 1) * 3], start=(k == 0), stop=(k == KC - 1)),
                        reads=[("WP", s), ("C3B",)], writes=[("PS", 0)],
                        signal=(k == KC - 1 and m4 == 3), extra=st if (j == 0 and m4 == 0 and k == 0) else None)
        for r in range(3):
            op("dve", lambda r=r: nc.vector.tensor_tensor(
                out=MOD[:, :, r], in0=PS[0][:, 0:NMOD * KC * 3].rearrange("p (c r) -> p c r", r=3)[:, :, r],
                in1=BADA[:], op=ALU.add), reads=[("PS", 0)], writes=[("MOD",), ("TMP", 0)])
        for s3 in range(3):
            gsel = GD[:, (0 if s3 == 0 else (1 if s3 == 1 else 2)) * KC:(0 if s3 == 0 else (1 if s3 == 1 else 2)) * KC + KC]
            for r in range(3):
                base = 3 * s3 * KC
                op("dve", lambda base=base, r=r: nc.vector.tensor_copy(
                    out=GS[:, base:base + KC, r], in_=MOD[:, base:base + KC, r]),
                    reads=[("MOD",), ("TMP", 0)], writes=[("GS",)])
                op("dve", lambda base=base, r=r, gsel=gsel: nc.vector.scalar_tensor_tensor(
                    out=GS[:, base + KC:base + 2 * KC, r], in0=MOD[:, base + KC:base + 2 * KC, r], scalar=1.0,
                    in1=gsel, op0=ALU.add, op1=ALU.mult), reads=[("MOD",)], writes=[("GS",)])
                op("dve", lambda base=base, r=r, s3=s3: nc.vector.tensor_scalar(
                    out=GS[:, base + 2 * KC:base + 3 * KC, r], in0=MOD[:, base + 2 * KC:base + 3 * KC, r],
                    scalar1=(1.0 if s3 == 1 else 0.5), scalar2=None, op0=ALU.mult),
                    reads=[("MOD",), ("TMP", 0)], writes=[("GS",)])
        gs_tok = sc.lastw[("GS",)]
        for e in ("act", "dve", "pe", "pool"):
            sc.wait_tok(e, st + [gs_tok])

        def SH(s3, k, r):
            return GS[:, 3 * s3 * KC + k, r:r + 1]

        def GG(s3, k, r):
            return GS[:, (3 * s3 + 1) * KC + k, r:r + 1]

        def GT(s3, k, r):
            return GS[:, (3 * s3 + 2) * KC + k, r:r + 1]

        st_rot = {"n": 0}
        mm_rot = {"n": 0}
        sq_rot = {"n": 0}
        tmp_rot = {"n": 0}
        sg_rot = {"n": 0}
        tm_rot = {"n": 0}
        cvb_rot = {"n": 0}

        def mm_bank():
            b = mm_rot["n"] % 4
            mm_rot["n"] += 1
            return b

        def load_block(tile_idx, bidx):
            seq = ring["n"]; ring["n"] += 1
            s = slot_of(seq)
            n = 5632 if blocks[bidx][0] == "DN" else SLOT
            owner = 0 if 3 * bidx < NBLK else 1
            if tile_idx <= owner:
                for dst, src in conv_srcs(bidx, WP[:, s, :]):
                    sc.dma("pool", dst, src, slot_sem_sw[s], writes=[("WP", s)])
                if tile_idx == owner:
                    sc.dma("sp", scr[bidx][:, 0:n], WP[:, s, 0:n], store_sem[s], reads=[("WP", s)],
                           writes=[("SCR", bidx)])
            else:
                sc.dma("sp", WP[:, s, 0:n], scr[bidx][:, 0:n], slot_sem[s], reads=[("SCR", bidx)],
                       writes=[("WP", s)])
            return s

        def rstd_from_sum(bank, wd, n_feat, np_=128):
            op("act", lambda: nc.scalar.activation(out=PS[bank][0:np_, 0:wd], in_=PS[bank][0:np_, 0:wd],
                                                   func=AF.Sqrt, bias=EPS, scale=1.0 / n_feat),
               reads=[("PS", bank)], writes=[("PS", bank)])
            op("dve", lambda: nc.vector.reciprocal(out=PS[bank][0:np_, 0:wd], in_=PS[bank][0:np_, 0:wd]),
               reads=[("PS", bank)], writes=[("PS", bank)])

        def sumsq(srcs, wd, bank=None):
            if bank is None:
                bank = 4 + (st_rot["n"] % 2)
                st_rot["n"] += 1
            n = len(srcs)
            for i, (ap, regs) in enumerate(srcs):
                q = sq_rot["n"] % 4
                sq_rot["n"] += 1
                op("act", lambda ap=ap, q=q: nc.scalar.activation(out=SQ[:, q, 0:wd], in_=ap, func=AF.Square),
                   reads=regs, writes=[("SQ", q)])
                op("pe", lambda q=q, i=i: nc.tensor.matmul(PS[bank][:, 0:wd], lhsT=ONES[:], rhs=SQ[:, q, 0:wd],
                                                           start=(i == 0), stop=(i == n - 1)),
                   reads=[("SQ", q)], writes=[("PS", bank)], signal=True)
            return bank

        class StatAcc:
            def __init__(self, wd, n_total, lag=2):
                self.bank = 4 + (st_rot["n"] % 2)
                st_rot["n"] += 1
                self.wd = wd
                self.n = n_total
                self.i = 0
                self.lag = lag
                self.pending = []

            def add(self, ap, regs):
                q = sq_rot["n"] % 4
                sq_rot["n"] += 1
                wd_ = self.wd
                op("act", lambda: nc.scalar.activation(out=SQ[:, q, 0:wd_], in_=ap, func=AF.Square),
                   reads=regs, writes=[("SQ", q)])
                self.pending.append(q)
                if len(self.pending) > self.lag:
                    self._mm(self.pending.pop(0))

            def _mm(self, q):
                i, n, bank, wd_ = self.i, self.n, self.bank, self.wd
                op("pe", lambda: nc.tensor.matmul(PS[bank][:, 0:wd_], lhsT=ONES[:], rhs=SQ[:, q, 0:wd_],
                                                  start=(i == 0), stop=(i == n - 1)),
                   reads=[("SQ", q)], writes=[("PS", bank)], signal=True)
                self.i += 1

            def finish(self):
                while self.pending:
                    self._mm(self.pending.pop(0))
                assert self.i == self.n
                return self.bank

        def norm_mod(s3, r, wd, bank=None):
            if bank is None:
                bank = sumsq([(XF[:, f, 0:wd], [("XF", f)]) for f in range(KC)], wd)
            rstd_from_sum(bank, wd, D)
            for f in range(KC):
                t = tmp_rot["n"] % 2
                tmp_rot["n"] += 1
                op("dve", lambda f=f, t=t: nc.vector.tensor_tensor(out=TMP[:, t, 0:wd], in0=XF[:, f, 0:wd],
                                                                   in1=PS[bank][:, 0:wd], op=ALU.mult),
                   reads=[("XF", f), ("PS", bank)], writes=[("TMP", t)])
                op("act", lambda f=f, t=t: nc.scalar.activation(out=H[:, f, 0:wd], in_=TMP[:, t, 0:wd],
                                                                func=AF.Identity, bias=SH(s3, f, r),
                                                                scale=GG(s3, f, r)),
                   reads=[("TMP", t)], writes=[("H", f)])

        def ffn(f_idx, s3, r, wd, tile_idx, bpos):
            sacc = None
            for half in range(2):
                if wd < 128:
                    pend = []

                    def flush_tm(item):
                        bi_, t_ = item
                        bk2 = mm_bank()
                        for h2 in range(2):
                            op("pe", lambda h2=h2: nc.tensor.transpose(
                                out=PS[bk2][:, h2 * 32:h2 * 32 + wd], in_=TMP[0:wd, t_, h2 * 128:(h2 + 1) * 128],
                                identity=IDENT[0:wd, 0:wd]),
                                reads=[("TMP", t_)], writes=[("PS", bk2)], signal=(h2 == 1))
                        j0 = bi_ * 2
                        op("act", lambda: nc.scalar.copy(
                            out=HID[:, j0 * 512:(j0 + 2) * 512].rearrange("p (a b) -> p a b", b=512)[:, :, 0:wd],
                            in_=PS[bk2][:, 0:64].rearrange("p (a b) -> p a b", b=32)[:, :, 0:wd]),
                            reads=[("PS", bk2)], writes=[("BA", j0), ("BA", j0 + 1)])

                    for bi in range(11):
                        s = load_block(tile_idx, bpos); bpos += 1
                        bank = mm_bank()
                        for k in range(KC):
                            op("pe", lambda s=s, k=k, bank=bank: nc.tensor.matmul(
                                PS[bank][0:wd, :], lhsT=H[:, k, 0:wd], rhs=WP[:, s, k * 512:(k + 1) * 512],
                                start=(k == 0), stop=(k == KC - 1)),
                                reads=[("WP", s), ("H", k)], writes=[("PS", bank)], signal=(k == KC - 1))
                        g = sg_rot["n"] % 2
                        sg_rot["n"] += 1
                        op("act", lambda g=g, bank=bank: nc.scalar.activation(
                            out=SG[0:wd, g, 0:256], in_=PS[bank][0:wd, 0:256], func=AF.Silu),
                            reads=[("PS", bank)], writes=[("SG", g)])
                        t = tmp_rot["n"] % 2
                        tmp_rot["n"] += 1
                        op("dve", lambda g=g, bank=bank, t=t: nc.vector.tensor_tensor(
                            out=TMP[0:wd, t, 0:256], in0=PS[bank][0:wd, 256:512], in1=SG[0:wd, g, 0:256],
                            op=ALU.mult),
                            reads=[("PS", bank), ("SG", g)], writes=[("TMP", t)])
                        if pend:
                            flush_tm(pend.pop(0))
                        pend.append((bi, t))
                    while pend:
                        flush_tm(pend.pop(0))
                for bi in range(11 if wd >= 128 else 0):
                    s = load_block(tile_idx, bpos); bpos += 1
                    for h2 in range(2):
                        j = bi * 2 + h2
                        bg = mm_bank()
                        bu = mm_bank()
                        for gu, bank in ((0, bg), (1, bu)):
                            for k in range(KC):
                                off = k * 512 + gu * 256 + h2 * 128
                                op("pe", lambda s=s, off=off, k=k, bank=bank: nc.tensor.matmul(
                                    PS[bank][:, 0:wd], lhsT=WP[:, s, off:off + 128], rhs=H[:, k, 0:wd],
                                    start=(k == 0), stop=(k == KC - 1)),
                                    reads=[("WP", s), ("H", k)], writes=[("PS", bank)], signal=(k == KC - 1))
                        g = sg_rot["n"] % 2
                        sg_rot["n"] += 1
                        op("act", lambda g=g, bg=bg: nc.scalar.activation(out=SG[:, g, 0:wd], in_=PS[bg][:, 0:wd],
                                                                          func=AF.Silu),
                           reads=[("PS", bg)], writes=[("SG", g)])
                        op("dve", lambda g=g, bu=bu, j=j: nc.vector.tensor_tensor(
                            out=HID[:, j * 512:j * 512 + wd], in0=PS[bu][:, 0:wd], in1=SG[:, g, 0:wd], op=ALU.mult),
                            reads=[("PS", bu), ("SG", g)], writes=[("BA", j)])
                if wd < 128:
                    pend_d = []

                    def flush_dn(item):
                        nonlocal sacc
                        mb_, t_ = item
                        bk2 = mm_bank()
                        for m2 in range(2):
                            op("pe", lambda m2=m2: nc.tensor.transpose(
                                out=PS[bk2][:, m2 * 32:m2 * 32 + wd], in_=TMP[0:wd, t_, m2 * 128:(m2 + 1) * 128],
                                identity=IDENT[0:wd, 0:wd]),
                                reads=[("TMP", t_)], writes=[("PS", bk2)], signal=(m2 == 1))
                        for m2 in range(2):
                            m = mb_ * 2 + m2
                            op("dve", lambda m=m, m2=m2: nc.vector.scalar_tensor_tensor(
                                out=XF[:, m, 0:wd], in0=PS[bk2][:, m2 * 32:m2 * 32 + wd], scalar=GT(s3, m, r),
                                in1=XF[:, m, 0:wd], op0=ALU.mult, op1=ALU.add),
                                reads=[("PS", bk2), ("XF", m)], writes=[("XF", m)])
                            if half == 1:
                                if sacc is None:
                                    sacc = StatAcc(wd, KC)
                                sacc.add(XF[:, m, 0:wd], [("XF", m)])

                    for mb in range(8):
                        s = load_block(tile_idx, bpos); bpos += 1
                        bank = mm_bank()
                        for k in range(22):
                            op("pe", lambda s=s, k=k, bank=bank: nc.tensor.matmul(
                                PS[bank][0:wd, 0:256], lhsT=HID[:, k * 512:k * 512 + wd],
                                rhs=WP[:, s, k * 256:(k + 1) * 256], start=(k == 0), stop=(k == 21)),
                                reads=[("WP", s), ("BA", k)], writes=[("PS", bank)], signal=(k == 21))
                        t = tmp_rot["n"] % 2
                        tmp_rot["n"] += 1
                        op("act", lambda t=t, bank=bank: nc.scalar.copy(out=TMP[0:wd, t, 0:256],
                                                                        in_=PS[bank][0:wd, 0:256]),
                           reads=[("PS", bank)], writes=[("TMP", t)])
                        if pend_d:
                            flush_dn(pend_d.pop(0))
                        pend_d.append((mb, t))
                    while pend_d:
                        flush_dn(pend_d.pop(0))
                for mb in range(8 if wd >= 128 else 0):
                    s = load_block(tile_idx, bpos); bpos += 1
                    for m2 in range(2):
                        m = mb * 2 + m2
                        bank = mm_bank()
                        for k in range(22):
                            off = k * 256 + m2 * 128
                            op("pe", lambda s=s, off=off, k=k, bank=bank: nc.tensor.matmul(
                                PS[bank][:, 0:wd], lhsT=WP[:, s, off:off + 128], rhs=HID[:, k * 512:k * 512 + wd],
                                start=(k == 0), stop=(k == 21)),
                                reads=[("WP", s), ("BA", k)], writes=[("PS", bank)], signal=(k == 21))
                        op("dve", lambda m=m, bank=bank: nc.vector.scalar_tensor_tensor(
                            out=XF[:, m, 0:wd], in0=PS[bank][:, 0:wd], scalar=GT(s3, m, r), in1=XF[:, m, 0:wd],
                            op0=ALU.mult, op1=ALU.add),
                            reads=[("PS", bank), ("XF", m)], writes=[("XF", m)])
                        if half == 1:
                            if sacc is None:
                                sacc = StatAcc(wd, KC)
                            sacc.add(XF[:, m, 0:wd], [("XF", m)])
            return bpos, sacc.finish()

        out_toks = []

        def tile(tile_idx, r, x_rows, y_rows, wd, first_in_seq, last_in_seq, is_sample, scp_dst):
            nrb = max(1, wd // 128)
            rows = min(wd, 128)
            bpos = 0
            use_pool = tile_idx >= pool_from
            HS = H.bitcast(F32).rearrange("p k w -> p (k w)")
            sacc0 = None
            for rb in range(nrb):
                if rb < 2:
                    sc.dma("sp", GLU[0:rows, 4 * rb:4 * rb + 4, 0:512],
                           x_rows[rb * rows:(rb + 1) * rows, :].rearrange("t (c f) -> t c f", c=4), tmi_sem[rb],
                           writes=[("GLU", 4 * rb + i) for i in range(4)])
                else:
                    o0 = (rb - 2) * D
                    sc.dma("sp", HS[0:rows, o0:o0 + D], x_rows[rb * rows:(rb + 1) * rows, :], tmi_sem[rb],
                           writes=[("H", 8 * (rb - 2) + i) for i in range(8)])
                for q in range(4):
                    bank = mm_bank()
                    for f4 in range(4):
                        if rb < 2:
                            src = GLU[0:rows, 4 * rb + q, f4 * 128:(f4 + 1) * 128]
                            regs = [("GLU", 4 * rb + q)]
                        else:
                            o1 = (rb - 2) * D + (q * 4 + f4) * 128
                            src = HS[0:rows, o1:o1 + 128]
                            regs = [("H", 8 * (rb - 2) + 2 * q + f4 // 2)]
                        op("pe", lambda src=src, f4=f4, bank=bank: nc.tensor.transpose(
                            out=PS[bank][:, f4 * 128:f4 * 128 + rows], in_=src, identity=IDENT[0:rows, 0:rows]),
                            reads=regs, writes=[("PS", bank)], signal=(f4 == 3))
                    xdst = XF[:, q * 4:(q + 1) * 4, rb * 128:rb * 128 + rows]
                    xsrc = PS[bank][:, :].rearrange("p (a b) -> p a b", b=128)[:, :, 0:rows]
                    xregs = [("XF", q * 4 + i) for i in range(4)]
                    if q % 2 == 0:
                        op("act", lambda xdst=xdst, xsrc=xsrc: nc.scalar.copy(out=xdst, in_=xsrc),
                           reads=[("PS", bank)], writes=xregs)
                    else:
                        op("dve", lambda xdst=xdst, xsrc=xsrc: nc.vector.tensor_copy(out=xdst, in_=xsrc),
                           reads=[("PS", bank)], writes=xregs)
                    if rb == nrb - 1:
                        if sacc0 is None:
                            sacc0 = StatAcc(wd, KC)
                        for i in range(4):
                            sacc0.add(XF[:, q * 4 + i, 0:wd], [("XF", q * 4 + i)])
            norm_mod(0, r, wd, bank=sacc0.finish())
            bpos, sbank = ffn(1, 0, r, wd, tile_idx, bpos)
            norm_mod(1, r, wd, bank=sbank)
            if first_in_seq:
                if is_sample:
                    sc.dma("sp", BIGA[0:HALO, 0:DB], cache, tmi_sem[0], writes=[("BA", i) for i in range(4)])
                    bank = mm_bank()
                    for c in range(8):
                        op("pe", lambda c=c, bank=bank: nc.tensor.transpose(
                            out=PS[bank][:, c * 32:c * 32 + HALO], in_=BIGA[0:HALO, c * 128:(c + 1) * 128],
                            identity=IDENT[0:HALO, 0:HALO]),
                            reads=[("BA", i) for i in range(4)], writes=[("PS", bank)], signal=(c == 7))
                    op("act", lambda bank=bank: nc.scalar.copy(
                        out=GLU[:, :, 0:HALO],
                        in_=PS[bank][:, 0:256].rearrange("p (a b) -> p a b", b=32)[:, :, 0:HALO]),
                        reads=[("PS", bank)], writes=[("GLU", c) for c in range(8)])
                else:
                    op("dve", lambda: nc.vector.memset(GLU[:, :, 0:HALO], 0.0),
                       writes=[("GLU", c) for c in range(8)])
            else:
                op("dve", lambda: nc.vector.tensor_copy(out=GLU[:, :, 0:HALO], in_=GLU[:, :, W:W + HALO]),
                   reads=[("GLU", c) for c in range(8)], writes=[("GLU", c) for c in range(8)])
            CVo = 4096

            def st_gab(nb):
                nonlocal bpos
                s = load_block(tile_idx, bpos); bpos += 1
                for h2 in range(2):
                    c = nb * 2 + h2
                    ba = mm_bank()
                    bb = mm_bank()
                    for gu, bank in ((0, ba), (1, bb)):
                        for k in range(KC):
                            off = k * 512 + gu * 256 + h2 * 128
                            op("pe", lambda s=s, off=off, k=k, bank=bank: nc.tensor.matmul(
                                PS[bank][:, 0:wd], lhsT=WP[:, s, off:off + 128], rhs=H[:, k, 0:wd],
                                start=(k == 0), stop=(k == KC - 1)),
                                reads=[("WP", s), ("H", k)], writes=[("PS", bank)], signal=(k == KC - 1))
                    g = sg_rot["n"] % 2
                    sg_rot["n"] += 1
                    op("act", lambda g=g, bb=bb: nc.scalar.activation(out=SG[:, g, 0:wd], in_=PS[bb][:, 0:wd],
                                                                      func=AF.Sigmoid),
                       reads=[("PS", bb)], writes=[("SG", g)])
                    if use_pool:
                        op("act", lambda ba=ba, c=c: nc.scalar.copy(out=GLU[:, c, HALO:HALO + wd],
                                                                    in_=PS[ba][:, 0:wd]),
                           reads=[("PS", ba)], writes=[("GLU", c)])
                        op("pool", lambda g=g, c=c: nc.gpsimd.tensor_tensor(
                            out=GLU[:, c, HALO:HALO + wd], in0=GLU[:, c, HALO:HALO + wd], in1=SG[:, g, 0:wd],
                            op=ALU.mult),
                            reads=[("GLU", c), ("SG", g)], writes=[("GLU", c)])
                    else:
                        op("dve", lambda g=g, ba=ba, c=c: nc.vector.tensor_tensor(
                            out=GLU[:, c, HALO:HALO + wd], in0=PS[ba][:, 0:wd], in1=SG[:, g, 0:wd], op=ALU.mult),
                            reads=[("PS", ba), ("SG", g)], writes=[("GLU", c)])
                    if c < 4:
                        op("act", lambda c=c: nc.scalar.copy(out=GLUB[:, c, 0:HALO + wd], in_=GLU[:, c, 0:HALO + wd]),
                           reads=[("GLU", c)], writes=[("GLUB", c)])

            def st_inu(ub):
                nonlocal bpos
                s = load_block(tile_idx, bpos); bpos += 1
                for m4 in range(4):
                    g = ub * 4 + m4
                    bank = mm_bank()
                    for k in range(KC):
                        off = k * 512 + m4 * 128
                        op("pe", lambda s=s, off=off, k=k, bank=bank: nc.tensor.matmul(
                            PS[bank][:, 0:wd], lhsT=WP[:, s, off:off + 128], rhs=H[:, k, 0:wd],
                            start=(k == 0), stop=(k == KC - 1)),
                            reads=[("WP", s), ("H", k)], writes=[("PS", bank)], signal=(k == KC - 1))
                    op("act", lambda g=g, bank=bank: nc.scalar.copy(out=UB[:, g, 0:wd], in_=PS[bank][:, 0:wd]),
                       reads=[("PS", bank)], writes=[("UB", g)])

            vt_box = {}

            def st_inv():
                nonlocal bpos
                sv = [load_block(tile_idx, bpos), load_block(tile_idx, bpos + 1)]
                bpos += 2
                for tb in range(nrb):
                    vbanks = []
                    for vb in range(2):
                        bank = mm_bank()
                        vbanks.append(bank)
                        s = sv[vb]
                        for k in range(KC):
                            op("pe", lambda s=s, k=k, bank=bank, tb=tb: nc.tensor.matmul(
                                PS[bank][0:rows, :], lhsT=H[:, k, tb * 128:tb * 128 + rows],
                                rhs=WP[:, s, k * 512:(k + 1) * 512], start=(k == 0), stop=(k == KC - 1)),
                                reads=[("WP", s), ("H", k)], writes=[("PS", bank)], signal=(k == KC - 1))
                    for vb in range(2):
                        q = sq_rot["n"] % 4
                        sq_rot["n"] += 1
                        op("act", lambda vb=vb, q=q, vbanks=vbanks: nc.scalar.activation(
                            out=SQ[0:rows, q, :], in_=PS[vbanks[vb]][0:rows, :], func=AF.Square,
                            accum_out=SS[0:rows, vb:vb + 1]),
                            reads=[("PS", vbanks[vb])], writes=[("SQ", q), ("SS",)])
                    if use_pool:
                        op("pool", lambda: nc.gpsimd.tensor_tensor(out=SS[0:rows, 2:3], in0=SS[0:rows, 0:1],
                                                                    in1=SS[0:rows, 1:2], op=ALU.add),
                           reads=[("SS",)], writes=[("SS2",)])
                        op("pool", lambda: nc.gpsimd.tensor_scalar(out=SS[0:rows, 3:4], in0=SS[0:rows, 2:3],
                                                                    scalar1=1.0 / DA, scalar2=EPS, op0=ALU.mult,
                                                                    op1=ALU.add),
                           reads=[("SS2",)], writes=[("SS3",)])
                        op("pool", lambda: nc.gpsimd.tensor_tensor(out=SS[0:rows, 4:5], in0=SS[0:rows, 3:4],
                                                                    in1=NEGH[0:rows, 0:1], op=ALU.pow),
                           reads=[("SS3",)], writes=[("SS4",)])
                        for vb in range(2):
                            g = sg_rot["n"] % 2
                            sg_rot["n"] += 1
                            op("act", lambda vb=vb, g=g, vbanks=vbanks: nc.scalar.activation(
                                out=SG[0:rows, g, :], in_=PS[vbanks[vb]][0:rows, :], func=AF.Identity,
                                scale=SS[0:rows, 4:5]),
                                reads=[("PS", vbanks[vb]), ("SS4",)], writes=[("SG", g)])
                            op("pool", lambda vb=vb, tb=tb, g=g: nc.gpsimd.tensor_tensor(
                                out=VN[0:rows, tb, vb * 512:(vb + 1) * 512], in0=SG[0:rows, g, :],
                                in1=GVB[0:rows, vb * 512:(vb + 1) * 512], op=ALU.mult),
                                reads=[("SG", g)], writes=[("VN", tb)])
                            if is_sample:
                                vt = 0
                                op("pool", lambda vb=vb, vt=vt, g=g: nc.gpsimd.tensor_tensor(
                                    out=BIGA[0:rows, DA + vb * 512:DA + (vb + 1) * 512], in0=SG[0:rows, g, :],
                                    in1=GVB[0:rows, vb * 512:(vb + 1) * 512], op=ALU.mult),
                                    reads=[("SG", g)], writes=[("BA", 4 + 2 * vb), ("BA", 5 + 2 * vb)])
                    else:
                        op("dve", lambda: nc.vector.tensor_tensor(out=SS[0:rows, 2:3], in0=SS[0:rows, 0:1],
                                                                  in1=SS[0:rows, 1:2], op=ALU.add),
                           reads=[("SS",)], writes=[("SS2",)])
                        op("act", lambda: nc.scalar.activation(out=SS[0:rows, 3:4], in_=SS[0:rows, 2:3],
                                                               func=AF.Sqrt, bias=EPS, scale=1.0 / DA),
                           reads=[("SS2",)], writes=[("SS3",)])
                        op("dve", lambda: nc.vector.reciprocal(out=SS[0:rows, 4:5], in_=SS[0:rows, 3:4]),
                           reads=[("SS3",)], writes=[("SS4",)])
                        for vb in range(2):
                            op("dve", lambda vb=vb, tb=tb, vbanks=vbanks: nc.vector.scalar_tensor_tensor(
                                out=VN[0:rows, tb, vb * 512:(vb + 1) * 512], in0=PS[vbanks[vb]][0:rows, :],
                                scalar=SS[0:rows, 4:5], in1=GVB[0:rows, vb * 512:(vb + 1) * 512],
                                op0=ALU.mult, op1=ALU.mult),
                                reads=[("PS", vbanks[vb]), ("SS4",)], writes=[("VN", tb)])
                            if is_sample:
                                vt = 0
                                op("dve", lambda vb=vb, vt=vt, vbanks=vbanks: nc.vector.scalar_tensor_tensor(
                                    out=BIGA[0:rows, DA + vb * 512:DA + (vb + 1) * 512], in0=PS[vbanks[vb]][0:rows, :],
                                    scalar=SS[0:rows, 4:5], in1=GVB[0:rows, vb * 512:(vb + 1) * 512],
                                    op0=ALU.mult, op1=ALU.mult),
                                    reads=[("PS", vbanks[vb]), ("SS4",)], writes=[("BA", 4 + 2 * vb), ("BA", 5 + 2 * vb)])
                if is_sample:
                    out_toks.append(sc.dma("pool", svs, BIGA[0:DEC, DA:2 * DA], tmo_sem[5],
                                           reads=[("BA", i) for i in range(4, 8)]))

            def st_cstate():
                if not last_in_seq:
                    return
                bank = mm_bank()
                bank2 = mm_bank()
                for c in range(8):
                    bk = bank if c < 4 else bank2
                    op("pe", lambda c=c, bk=bk: nc.tensor.transpose(
                        out=PS[bk][0:HALO, (c % 4) * 128:(c % 4 + 1) * 128], in_=GLU[:, c, wd:wd + HALO],
                        identity=IDENT[:, :]),
                        reads=[("GLU", c)], writes=[("PS", bk)], signal=(c % 4 == 3))
                op("act", lambda: nc.scalar.copy(out=BIGA[0:HALO, 0:512], in_=PS[bank][0:HALO, :]),
                   reads=[("PS", bank)], writes=[("BA", 0), ("BA", 1)])
                op("act", lambda: nc.scalar.copy(out=BIGA[0:HALO, 512:1024], in_=PS[bank2][0:HALO, :]),
                   reads=[("PS", bank2)], writes=[("BA", 2), ("BA", 3)])
                out_toks.append(sc.dma("pool", scp_dst, BIGA[0:HALO, 0:DB], tmo_sem[4],
                                       reads=[("BA", i) for i in range(4)]))

            def st_mixa():
                for g in range(NG):
                    bank = mm_bank()
                    for tb in range(nrb):
                        op("pe", lambda g=g, tb=tb, bank=bank: nc.tensor.matmul(
                            PS[bank][:, tb * 128:tb * 128 + rows], lhsT=VN[0:rows, tb, g * 128:(g + 1) * 128],
                            rhs=WST[0:rows, g, 0:rows], start=True, stop=False),
                            reads=[("VN", tb)], writes=[("PS", bank)], signal=False)
                        op("pe", lambda g=g, tb=tb, bank=bank: nc.tensor.matmul(
                            PS[bank][:, tb * 128:tb * 128 + rows], lhsT=ONES[:, :],
                            rhs=BS128[:, g * 128:g * 128 + rows], start=False, stop=True),
                            reads=[], writes=[("PS", bank)], signal=(tb == nrb - 1))
                    op("dve", lambda g=g, bank=bank: nc.vector.tensor_tensor(
                        out=BIGA[:, g * 512:g * 512 + wd], in0=PS[bank][:, 0:wd], in1=UB[:, g, 0:wd],
                        op=ALU.mult),
                        reads=[("PS", bank), ("UB", g)], writes=[("BA", 2 * g), ("BA", 2 * g + 1)])

            conv_cb = {}

            dg_rot = {"n": 0}

            def conv_mm(c):
                bank = mm_bank()
                conv_cb[c] = bank
                for k in range(CW):
                    i = dg_rot["n"] % NDG
                    dg_rot["n"] += 1
                    wk = WDW[:, c * CW + k:c * CW + k + 1]
                    if use_pool:
                        op("pool", lambda: nc.gpsimd.tensor_scalar(out=DG[:, i, :], in0=IDB[:, :], scalar1=wk,
                                                                    scalar2=0.0, op0=ALU.mult, op1=ALU.add),
                           writes=[("DG", i)])
                    else:
                        op("act", lambda: nc.scalar.activation(out=DG[:, i, :], in_=IDB[:, :], func=AF.Identity,
                                                               scale=wk),
                           writes=[("DG", i)])
                    op("pe", lambda: nc.tensor.matmul(PS[bank][:, 0:wd], lhsT=DG[:, i, :], rhs=GLUB[:, c, k:k + wd],
                                                      start=(k == 0), stop=(k == CW - 1)),
                       reads=[("DG", i), ("GLUB", c)], writes=[("PS", bank)], signal=True)

            def conv_evac(c):
                cb = conv_cb[c]
                bdw = G8[:, c:c + 1]
                op("act", lambda: nc.scalar.activation(out=BIGA[:, CVo + c * 512:CVo + c * 512 + wd],
                                                       in_=PS[cb][:, 0:wd], func=AF.Identity, bias=bdw, scale=1.0),
                   reads=[("PS", cb)], writes=[("BA", 16 + 2 * c), ("BA", 17 + 2 * c)])
                x2 = c % 2
                op("act", lambda: nc.scalar.activation(out=XB[:, x2, 0:wd], in_=PS[cb][:, 0:wd], func=AF.Identity,
                                                       bias=bdw, scale=1.0),
                   reads=[("PS", cb)], writes=[("XB", x2)])
                q = sq_rot["n"] % 4
                sq_rot["n"] += 1
                conv_sq[c] = q
                op("act", lambda: nc.scalar.activation(out=SQ[:, q, 0:wd], in_=PS[cb][:, 0:wd], func=AF.Square,
                                                       bias=bdw, scale=1.0),
                   reads=[("PS", cb)], writes=[("SQ", q)])

            def conv_taps2(c0, c1):
                cbs = {}
                for c in (c0, c1):
                    cbs[c] = 6 + (cvb_rot["n"] % 2)
                    cvb_rot["n"] += 1
                    conv_cb[c] = cbs[c]
                for k in range(CW):
                    for c in (c0, c1):
                        cb = cbs[c]
                        if k == 0:
                            op("dve", lambda: nc.vector.tensor_scalar(
                                out=PS[cb][:, 0:wd], in0=GLU[:, c, 0:wd], scalar1=WDW[:, c * CW:c * CW + 1],
                                scalar2=G8[:, c:c + 1], op0=ALU.mult, op1=ALU.add),
                                reads=[("GLU", c)], writes=[("PS", cb)])
                        else:
                            op("dve", lambda: nc.vector.scalar_tensor_tensor(
                                out=PS[cb][:, 0:wd], in0=GLU[:, c, k:k + wd],
                                scalar=WDW[:, c * CW + k:c * CW + k + 1],
                                in1=PS[cb][:, 0:wd], op0=ALU.mult, op1=ALU.add),
                                reads=[("GLU", c), ("PS", cb)], writes=[("PS", cb)])

            def conv_evac_plain(c):
                cb = conv_cb[c]
                op("act", lambda: nc.scalar.copy(out=BIGA[:, CVo + c * 512:CVo + c * 512 + wd], in_=PS[cb][:, 0:wd]),
                   reads=[("PS", cb)], writes=[("BA", 16 + 2 * c), ("BA", 17 + 2 * c)])
                x2 = c % 2
                op("act", lambda: nc.scalar.copy(out=XB[:, x2, 0:wd], in_=PS[cb][:, 0:wd]),
                   reads=[("PS", cb)], writes=[("XB", x2)])
                q = sq_rot["n"] % 4
                sq_rot["n"] += 1
                conv_sq[c] = q
                op("act", lambda: nc.scalar.activation(out=SQ[:, q, 0:wd], in_=PS[cb][:, 0:wd], func=AF.Square),
                   reads=[("PS", cb)], writes=[("SQ", q)])

            n_stats = {"n": 0}

            conv_sq = {}

            def conv_stats(c):
                x2 = c % 2
                q = conv_sq[c]
                first = n_stats["n"] == 0
                last = n_stats["n"] == 7
                n_stats["n"] += 1
                op("pe", lambda: nc.tensor.matmul(PS[4][:, 0:wd], lhsT=ONES[:], rhs=XB[:, x2, 0:wd],
                                                  start=first, stop=last),
                   reads=[("XB", x2)], writes=[("PS", 4)], signal=True)
                op("pe", lambda: nc.tensor.matmul(PS[5][:, 0:wd], lhsT=ONES[:], rhs=SQ[:, q, 0:wd],
                                                  start=first, stop=last),
                   reads=[("SQ", q)], writes=[("PS", 5)], signal=True)

            def ln_tail():
                t0 = tmp_rot["n"] % 2
                tmp_rot["n"] += 1
                op("act", lambda: nc.scalar.activation(out=TMP[:, t0, 0:wd], in_=PS[4][:, 0:wd], func=AF.Square,
                                                       scale=1.0 / DB),
                   reads=[("PS", 4)], writes=[("TMP", t0)])
                op("act", lambda: nc.scalar.mul(out=PS[4][:, 0:wd], in_=PS[4][:, 0:wd], mul=1.0 / DB),
                   reads=[("PS", 4)], writes=[("PS", 4)])
                op("dve", lambda: nc.vector.scalar_tensor_tensor(out=PS[5][:, 0:wd], in0=PS[5][:, 0:wd], scalar=1.0 / DB,
                                                                 in1=TMP[:, t0, 0:wd], op0=ALU.mult, op1=ALU.subtract),
                   reads=[("PS", 5), ("TMP", t0)], writes=[("PS", 5)])
                op("act", lambda: nc.scalar.activation(out=PS[5][:, 0:wd], in_=PS[5][:, 0:wd], func=AF.Sqrt,
                                                       bias=EPS, scale=1.0),
                   reads=[("PS", 5)], writes=[("PS", 5)])
                op("dve", lambda: nc.vector.reciprocal(out=PS[5][:, 0:wd], in_=PS[5][:, 0:wd]),
                   reads=[("PS", 5)], writes=[("PS", 5)])
                ybs = []
                for c in range(8):
                    cvr = [("BA", 16 + 2 * c), ("BA", 17 + 2 * c)]
                    cv = BIGA[:, CVo + c * 512:CVo + c * 512 + wd]
                    t1 = tmp_rot["n"] % 2
                    tmp_rot["n"] += 1
                    op("dve", lambda cv=cv, t1=t1: nc.vector.tensor_tensor(out=TMP[:, t1, 0:wd], in0=cv,
                                                                           in1=PS[4][:, 0:wd], op=ALU.subtract),
                       reads=cvr + [("PS", 4)], writes=[("TMP", t1)])
                    op("dve", lambda t1=t1: nc.vector.tensor_tensor(out=TMP[:, t1, 0:wd], in0=TMP[:, t1, 0:wd],
                                                                    in1=PS[5][:, 0:wd], op=ALU.mult),
                       reads=[("TMP", t1), ("PS", 5)], writes=[("TMP", t1)])
                    op("act", lambda cv=cv, t1=t1, c=c: nc.scalar.activation(
                        out=cv, in_=TMP[:, t1, 0:wd], func=AF.Silu, bias=G8[:, 16 + c:17 + c], scale=G8[:, 8 + c:9 + c]),
                        reads=[("TMP", t1)], writes=cvr)
                    ybs.append((cv, cvr))
                return ybs

            def ln_tail2(ybs):
                sumsq(ybs, wd, bank=4)
                rstd_from_sum(4, wd, DB)
                for c in range(8):
                    cv, cvr = ybs[c]
                    op("dve", lambda c=c, cv=cv: nc.vector.scalar_tensor_tensor(
                        out=H[:, 8 + c, 0:wd], in0=cv, scalar=G8[:, 32 + c:33 + c], in1=PS[4][:, 0:wd],
                        op0=ALU.mult, op1=ALU.mult),
                        reads=cvr + [("PS", 4)], writes=[("H", 8 + c)])

            st_gab(2)
            conv_taps2(4, 5)
            st_gab(3)
            st_gab(0)
            conv_evac_plain(4)
            conv_evac_plain(5)
            conv_taps2(6, 7)
            st_gab(1)
            conv_stats(4)
            conv_stats(5)
            conv_mm(0); conv_evac(0)
            conv_mm(1); conv_evac(1); conv_stats(0)
            conv_mm(2); conv_evac(2); conv_stats(1)
            conv_mm(3); conv_evac(3); conv_stats(2)
            st_inu(0)
            conv_stats(3)
            conv_evac_plain(6); conv_stats(6)
            conv_evac_plain(7); conv_stats(7)
            ybs_ = ln_tail()
            st_inu(1)
            st_inv()
            ln_tail2(ybs_)
            st_cstate()
            st_mixa()
            bank = sumsq([(BIGA[:, g * 512:g * 512 + wd], [("BA", 2 * g), ("BA", 2 * g + 1)]) for g in range(NG)], wd,
                         bank=5)
            rstd_from_sum(bank, wd, DA)
            for g in range(NG):
                op("dve", lambda g=g, bank=bank: nc.vector.scalar_tensor_tensor(
                    out=H[:, g, 0:wd], in0=BIGA[:, g * 512:g * 512 + wd], scalar=G8[:, 24 + g:25 + g],
                    in1=PS[bank][:, 0:wd], op0=ALU.mult, op1=ALU.mult),
                    reads=[("BA", 2 * g), ("BA", 2 * g + 1), ("PS", bank)], writes=[("H", g)])
            sacc3 = None
            for ob in range(4):
                s = load_block(tile_idx, bpos); bpos += 1
                for m4 in range(4):
                    m = ob * 4 + m4
                    bank = mm_bank()
                    for k in range(KC):
                        off = k * 512 + m4 * 128
                        op("pe", lambda s=s, off=off, k=k, bank=bank: nc.tensor.matmul(
                            PS[bank][:, 0:wd], lhsT=WP[:, s, off:off + 128], rhs=H[:, k, 0:wd],
                            start=(k == 0), stop=(k == KC - 1)),
                            reads=[("WP", s), ("H", k)], writes=[("PS", bank)], signal=(k == KC - 1))
                    op("dve", lambda m=m, bank=bank: nc.vector.scalar_tensor_tensor(
                        out=XF[:, m, 0:wd], in0=PS[bank][:, 0:wd], scalar=GT(1, m, r), in1=XF[:, m, 0:wd],
                        op0=ALU.mult, op1=ALU.add),
                        reads=[("PS", bank), ("XF", m)], writes=[("XF", m)])
                    if sacc3 is None:
                        sacc3 = StatAcc(wd, KC)
                    sacc3.add(XF[:, m, 0:wd], [("XF", m)])
            norm_mod(2, r, wd, bank=sacc3.finish())
            bpos, bank = ffn(2, 2, r, wd, tile_idx, bpos)
            assert bpos == NBLK
            rstd_from_sum(bank, wd, D)
            for f in range(KC):
                op("dve", lambda f=f, bank=bank: nc.vector.scalar_tensor_tensor(
                    out=XF[:, f, 0:wd], in0=XF[:, f, 0:wd], scalar=GD[:, 3 * KC + f:3 * KC + f + 1],
                    in1=PS[bank][:, 0:wd], op0=ALU.mult, op1=ALU.mult),
                    reads=[("XF", f), ("PS", bank)], writes=[("XF", f)])
            for rb in range(nrb):
                for q in range(4):
                    bk = mm_bank()
                    for f4 in range(4):
                        f = q * 4 + f4
                        op("pe", lambda f=f, f4=f4, bk=bk, rb=rb: nc.tensor.transpose(
                            out=PS[bk][0:rows, f4 * 128:(f4 + 1) * 128], in_=XF[:, f, rb * 128:rb * 128 + rows],
                            identity=IDENT[:, :]),
                            reads=[("XF", f)], writes=[("PS", bk)], signal=(f4 == 3))
                    o0 = rb * D + q * 512
                    regs = [("BA", rb * 8 + q * 2), ("BA", rb * 8 + q * 2 + 1)]
                    if q % 2 == 0:
                        op("act", lambda o0=o0, bk=bk: nc.scalar.copy(out=BIGA[0:rows, o0:o0 + 512],
                                                                      in_=PS[bk][0:rows, :]),
                           reads=[("PS", bk)], writes=regs)
                    else:
                        op("dve", lambda o0=o0, bk=bk: nc.vector.tensor_copy(out=BIGA[0:rows, o0:o0 + 512],
                                                                             in_=PS[bk][0:rows, :]),
                           reads=[("PS", bk)], writes=regs)
                out_toks.append(sc.dma("pool", y_rows[rb * rows:(rb + 1) * rows, :], BIGA[0:rows, rb * D:(rb + 1) * D],
                                       tmo_sem[rb], reads=[("BA", rb * 8 + i) for i in range(8)]))

        tidx = 0
        for b in range(2):
            for tt in range(NTB):
                tile(tidx, b, xp[b, tt * W:(tt + 1) * W, :], yp[b, tt * W:(tt + 1) * W, :], W,
                     first_in_seq=(tt == 0), last_in_seq=(tt == NTB - 1), is_sample=False, scp_dst=scp[b])
                tidx += 1
        tile(tidx, 2, xs, ys, DEC, first_in_seq=True, last_in_seq=True, is_sample=True, scp_dst=scs)

        final = {}
        for s, v in out_toks:
            final[s] = max(final.get(s, 0), v)
        for s, v in final.items():
            nc.gpsimd.wait_ge(sc.sems[s], v)
        nc.sync.wait_ge(sc.sems["pe"], sc.val["pe"])
        nc.scalar.wait_ge(sc.sems["pe"], sc.val["pe"])
        build.stats = dict(n_wait=sc.n_wait, vals=dict(sc.val))
    return nc


def _fm(v, nchunk):
    return np.ascontiguousarray(np.asarray(v, np.float32).reshape(nchunk, 128).T)


def make_in_maps(inp, S=2048, n_cores=N_CORES):
    f32 = lambda a: np.ascontiguousarray(np.asarray(a, dtype=np.float32))
    x_prompt = f32(inp["x_prompt"]); x_sample = f32(inp["x_sample"])
    cache = f32(inp["cache_conv"])[0]
    c_prompt = f32(inp["c_prompt"]); c_sample = f32(inp["c_sample"])
    w_ada = f32(inp["w_ada"])[0]
    b_ada = _fm(f32(inp["b_ada"])[0], NMOD * KC)
    gd = np.concatenate([_fm(f32(inp[k])[0] if k != "g_final" else f32(inp[k]), KC)
                         for k in ("g_ffn1", "g_mix", "g_ffn2", "g_final")], axis=1)
    g8 = np.concatenate([_fm(f32(inp[k])[0], 8) for k in ("b_dw", "g_cn", "b_cn", "g_out_a", "g_out_b")], axis=1)
    wdw = f32(inp["w_dw"])[0]
    wdw_fm = np.ascontiguousarray(wdw.reshape(CW, 8, 128).transpose(2, 1, 0).reshape(128, 8 * CW))
    gvb = np.ascontiguousarray(np.broadcast_to(f32(inp["g_v"])[0][None, :], (128, DA)))
    ws = f32(inp["w_s"])[0]
    wst = np.ascontiguousarray(ws.transpose(2, 0, 1).reshape(128, NG * 128))
    trilT = np.ascontiguousarray(np.triu(np.ones((128, 128), np.float32)))
    bsv = np.ascontiguousarray(f32(inp["b_s"])[0].reshape(1, NG * 128))
    ident = np.eye(128, dtype=np.float32)
    shared = dict(w_ada=w_ada, b_ada=b_ada, gd=np.ascontiguousarray(gd), g8=np.ascontiguousarray(g8), wdw=wdw_fm,
                  gvb=gvb, wst=wst, trilT=trilT, bs=bsv, ident=ident,
                  w_up1=f32(inp["w_up1"])[0], w_up2=f32(inp["w_up2"])[0],
                  w_down1=f32(inp["w_down1"])[0], w_down2=f32(inp["w_down2"])[0],
                  w_in=f32(inp["w_in"])[0], w_out=f32(inp["w_out"])[0])
    maps = []
    for c in range(n_cores):
        c3 = np.stack([c_prompt[2 * c], c_prompt[2 * c + 1], c_sample[c]], axis=0)
        c3t = np.ascontiguousarray(c3.reshape(3, KC, 128).transpose(2, 1, 0).reshape(128, KC * 3))
        m = dict(shared)
        m.update(xp=np.ascontiguousarray(x_prompt[2 * c:2 * c + 2, :S]), xs=np.ascontiguousarray(x_sample[c]),
                 cache=np.ascontiguousarray(cache[c]), c3t=c3t)
        maps.append(m)
    return maps


def kernel(**inputs):
    S = 2048
    nc = build(S)
    maps = make_in_maps(inputs, S)
    res = run_bass_kernel_spmd(nc, maps, core_ids=list(range(N_CORES)))
    rs = res.results
    y_prompt = np.concatenate([r["yp"] for r in rs], axis=0).astype(np.float32)
    y_sample = np.stack([r["ys"] for r in rs], axis=0).astype(np.float32)
    scp = np.concatenate([r["scp"] for r in rs], axis=0)[None].astype(np.float32)
    scs = np.stack([r["scs"] for r in rs], axis=0)[None].astype(np.float32)
    svs = np.stack([r["svs"] for r in rs], axis=0)[None].astype(np.float32)
    return (y_prompt, y_sample, scp, scs, svs)
```
